# Optimizing a Trainium2 kernel written in Bass

```python
import math
import jax, jax.numpy as jnp
from jax import lax
import numpy as np

D_MODEL = 1024
BATCH = 2
SEQ = 8192
DEPTH = 1
DEC_BATCH = 128
DEC_SEQ = 1
PAST_LEN = 16384
PAGE_SIZE = 128

N_META = 16
N_HEADS = 8
N_KV_HEADS = 2
HEAD_DIM = 64
Q_PER_KV = N_HEADS // N_KV_HEADS
ATTN_WIDTH = N_HEADS * HEAD_DIM
KV_WIDTH = N_KV_HEADS * HEAD_DIM
WINDOW = 128
BLOCK = 128
CONV_CH = 512
CONV_W = 31
REL_BUCKETS = 32
REL_MAX_DIST = 128
D_FF = 2816
EPS = 1e-6
NEG = -1e30
IN_SPLITS = (ATTN_WIDTH, KV_WIDTH, KV_WIDTH, 2 * CONV_CH, D_MODEL, D_MODEL)
IN_WIDTH = sum(IN_SPLITS)

kernel_name = "hybrid_swa_sink_conformer_conv_macaron_step"


def rms_norm(x, g):
    xf = x.astype(jnp.float32)
    y = xf * lax.rsqrt(jnp.mean(xf * xf, axis=-1, keepdims=True) + EPS)
    return (y * g.astype(jnp.float32)).astype(x.dtype)


def layer_norm(x, g, b):
    xf = x.astype(jnp.float32)
    mu = jnp.mean(xf, axis=-1, keepdims=True)
    var = jnp.mean(jnp.square(xf - mu), axis=-1, keepdims=True)
    y = (xf - mu) * lax.rsqrt(var + EPS)
    return (y * g.astype(jnp.float32) + b.astype(jnp.float32)).astype(x.dtype)


def swiglu_ffn(x, g, w1, w3, w2):
    h = rms_norm(x, g)
    return (jax.nn.silu(h @ w1) * (h @ w3)) @ w2


def t5_bucket(dist):
    max_exact = REL_BUCKETS // 2
    d = jnp.maximum(dist, 0)
    ratio = jnp.log(jnp.maximum(d, 1).astype(jnp.float32) / max_exact) / math.log(REL_MAX_DIST / max_exact)
    large = jnp.minimum(max_exact + (ratio * (REL_BUCKETS - max_exact)).astype(jnp.int32), REL_BUCKETS - 1)
    return jnp.where(d < max_exact, d, large)


def band_bias(dist, rel_bias):
    b = rel_bias.astype(jnp.float32)[t5_bucket(dist)]
    b = jnp.transpose(b, (2, 0, 1)).reshape(N_KV_HEADS, Q_PER_KV, *dist.shape)
    ok = (dist >= 0) & (dist < WINDOW)
    return jnp.where(ok, b, NEG)


def sink_attention(q, k, v, bias, sinks):
    s = jnp.einsum('...qkgd,...skd->...kgqs', q, k).astype(jnp.float32) * (HEAD_DIM ** -0.5) + bias
    sink = sinks.astype(jnp.float32).reshape(N_KV_HEADS, Q_PER_KV, 1)
    m = jnp.maximum(jnp.max(s, axis=-1), sink)
    p = jnp.exp(s - m[..., None])
    denom = jnp.sum(p, axis=-1) + jnp.exp(sink - m)
    p = (p / denom[..., None]).astype(v.dtype)
    return jnp.einsum('...kgqs,...skd->...qkgd', p, v)


def pre_mix(h, mix_norm, w_in, q_norm, k_norm):
    u = rms_norm(h, mix_norm)
    z = u @ w_in
    o = [0]
    for w in IN_SPLITS:
        o.append(o[-1] + w)
    lead = h.shape[:-1]
    q = rms_norm(z[..., o[0]:o[1]].reshape(*lead, N_HEADS, HEAD_DIM), q_norm)
    q = q.reshape(*lead, N_KV_HEADS, Q_PER_KV, HEAD_DIM)
    k = rms_norm(z[..., o[1]:o[2]].reshape(*lead, N_KV_HEADS, HEAD_DIM), k_norm)
    v = z[..., o[2]:o[3]].reshape(*lead, N_KV_HEADS, HEAD_DIM)
    a, b = jnp.split(z[..., o[3]:o[4]], 2, axis=-1)
    glu = a * jax.nn.sigmoid(b)
    gate_attn = jax.nn.sigmoid(z[..., o[4]:o[5]])
    gate_conv = jax.nn.sigmoid(z[..., o[5]:o[6]])
    return q, k, v, glu, gate_attn, gate_conv


def conv_branch(c, w_dw, b_dw, conv_ln_g, conv_ln_b, w_conv_out):
    y = lax.conv_general_dilated(c, w_dw[:, None, :].astype(c.dtype), window_strides=(1,), padding='VALID',
                                 dimension_numbers=('NWC', 'WIO', 'NWC'), feature_group_count=CONV_CH) + b_dw
    y = jax.nn.silu(layer_norm(y, conv_ln_g, conv_ln_b))
    return y @ w_conv_out


def post_mix(h, attn_o, conv_o, gate_attn, gate_conv, w_attn_out, w_out):
    a = attn_o.reshape(*attn_o.shape[:-3], ATTN_WIDTH) @ w_attn_out
    return h + (gate_attn * a + gate_conv * conv_o) @ w_out


def setup_inputs(seed: int = 0) -> dict:
    key = jax.random.key(seed)
    ks = jax.random.split(key, 32)
    nrm = lambda k, shape, s: jax.random.normal(k, shape, jnp.float32) * s
    gain = lambda k, n: 1.0 + 0.05 * jax.random.normal(k, (n,), jnp.float32)
    w_buf = min(WINDOW, PAST_LEN)
    return {
        "x_prompt": nrm(ks[0], (BATCH, SEQ, D_MODEL), 1.0),
        "x_sample": nrm(ks[1], (DEC_BATCH, DEC_SEQ, D_MODEL), 1.0),
        "cache_k": nrm(ks[2], (DEC_BATCH, w_buf, N_KV_HEADS, HEAD_DIM), 1.0),
        "cache_v": nrm(ks[3], (DEC_BATCH, w_buf, N_KV_HEADS, HEAD_DIM), 1.0),
        "state_conv": nrm(ks[4], (DEC_BATCH, CONV_W - 1, CONV_CH), 0.5),
        "meta_tokens": nrm(ks[5], (N_META, D_MODEL), 1.0),
        "ffn1_norm": gain(ks[6], D_MODEL),
        "ffn1_w1": nrm(ks[7], (D_MODEL, D_FF), D_MODEL ** -0.5),
        "ffn1_w3": nrm(ks[8], (D_MODEL, D_FF), D_MODEL ** -0.5),
        "ffn1_w2": nrm(ks[9], (D_FF, D_MODEL), D_FF ** -0.5),
        "mix_norm": gain(ks[10], D_MODEL),
        "w_in": nrm(ks[11], (D_MODEL, IN_WIDTH), D_MODEL ** -0.5),
        "q_norm": gain(ks[12], HEAD_DIM),
        "k_norm": gain(ks[13], HEAD_DIM),
        "rel_bias": nrm(ks[14], (REL_BUCKETS, N_HEADS), 0.5),
        "sinks": nrm(ks[15], (N_HEADS,), 0.5),
        "w_attn_out": nrm(ks[16], (ATTN_WIDTH, D_MODEL), ATTN_WIDTH ** -0.5),
        "w_dw": nrm(ks[17], (CONV_W, CONV_CH), CONV_W ** -0.5),
        "b_dw": nrm(ks[18], (CONV_CH,), 0.02),
        "conv_ln_g": gain(ks[19], CONV_CH),
        "conv_ln_b": nrm(ks[20], (CONV_CH,), 0.02),
        "w_conv_out": nrm(ks[21], (CONV_CH, D_MODEL), CONV_CH ** -0.5),
        "w_out": nrm(ks[22], (D_MODEL, D_MODEL), D_MODEL ** -0.5),
        "ffn2_norm": gain(ks[23], D_MODEL),
        "ffn2_w1": nrm(ks[24], (D_MODEL, D_FF), D_MODEL ** -0.5),
        "ffn2_w3": nrm(ks[25], (D_MODEL, D_FF), D_MODEL ** -0.5),
        "ffn2_w2": nrm(ks[26], (D_FF, D_MODEL), D_FF ** -0.5),
    }


def reference(x_prompt, x_sample, cache_k, cache_v, state_conv, meta_tokens,
              ffn1_norm, ffn1_w1, ffn1_w3, ffn1_w2, mix_norm, w_in, q_norm, k_norm,
              rel_bias, sinks, w_attn_out, w_dw, b_dw, conv_ln_g, conv_ln_b, w_conv_out,
              w_out, ffn2_norm, ffn2_w1, ffn2_w3, ffn2_w2):
    B = x_prompt.shape[0]
    x = jnp.concatenate([jnp.broadcast_to(meta_tokens.astype(x_prompt.dtype)[None], (B, N_META, D_MODEL)),
                         x_prompt], axis=1)
    L = x.shape[1]
    for _ in range(DEPTH):
        h = x + 0.5 * swiglu_ffn(x, ffn1_norm, ffn1_w1, ffn1_w3, ffn1_w2)
        q, k, v, glu, ga, gc = pre_mix(h, mix_norm, w_in, q_norm, k_norm)
        lead_pad = (-N_META) % BLOCK
        nb = (L + lead_pad) // BLOCK
        qb = jnp.pad(q, ((0, 0), (lead_pad, 0), (0, 0), (0, 0), (0, 0))).reshape(
            B, nb, BLOCK, N_KV_HEADS, Q_PER_KV, HEAD_DIM)
        kp = jnp.pad(k, ((0, 0), (lead_pad + BLOCK, 0), (0, 0), (0, 0))).reshape(B, nb + 1, BLOCK, N_KV_HEADS, HEAD_DIM)
        vp = jnp.pad(v, ((0, 0), (lead_pad + BLOCK, 0), (0, 0), (0, 0))).reshape(B, nb + 1, BLOCK, N_KV_HEADS, HEAD_DIM)
        kb = jnp.concatenate([kp[:, :-1], kp[:, 1:]], axis=2)
        vb = jnp.concatenate([vp[:, :-1], vp[:, 1:]], axis=2)
        dist = jnp.arange(BLOCK)[:, None] + BLOCK - jnp.arange(2 * BLOCK)[None, :]
        key_pos = (jnp.arange(nb)[:, None] * BLOCK + jnp.arange(2 * BLOCK)[None, :] - BLOCK - lead_pad)
        valid = jnp.where(key_pos >= 0, 0.0, NEG).astype(jnp.float32)
        bias = band_bias(dist, rel_bias)[None] + valid[:, None, None, None, :]
        attn_o = sink_attention(qb, kb, vb, bias, sinks).reshape(
            B, nb * BLOCK, N_KV_HEADS, Q_PER_KV, HEAD_DIM)[:, lead_pad:]
        c = jnp.pad(glu, ((0, 0), (CONV_W - 1, 0), (0, 0)))
        conv_o = conv_branch(c, w_dw, b_dw, conv_ln_g, conv_ln_b, w_conv_out)
        h2 = post_mix(h, attn_o, conv_o, ga, gc, w_attn_out, w_out)
        x = h2 + 0.5 * swiglu_ffn(h2, ffn2_norm, ffn2_w1, ffn2_w3, ffn2_w2)
        new_k_prompt = k[:, L - WINDOW:]
        new_v_prompt = v[:, L - WINDOW:]
        new_conv_prompt = glu[:, L - (CONV_W - 1):]
    y_prompt = x[:, N_META:]

    xs = x_sample
    T = xs.shape[1]
    w_buf = cache_k.shape[1]
    for _ in range(DEPTH):
        h = xs + 0.5 * swiglu_ffn(xs, ffn1_norm, ffn1_w1, ffn1_w3, ffn1_w2)
        q, k, v, glu, ga, gc = pre_mix(h, mix_norm, w_in, q_norm, k_norm)
        k_all = jnp.concatenate([cache_k.astype(k.dtype), k], axis=1)
        v_all = jnp.concatenate([cache_v.astype(v.dtype), v], axis=1)
        dist = jnp.arange(T)[:, None] + w_buf - jnp.arange(w_buf + T)[None, :]
        attn_o = sink_attention(q, k_all, v_all, band_bias(dist, rel_bias), sinks)
        c = jnp.concatenate([state_conv.astype(glu.dtype), glu], axis=1)
        conv_o = conv_branch(c, w_dw, b_dw, conv_ln_g, conv_ln_b, w_conv_out)
        h2 = post_mix(h, attn_o, conv_o, ga, gc, w_attn_out, w_out)
        xs = h2 + 0.5 * swiglu_ffn(h2, ffn2_norm, ffn2_w1, ffn2_w3, ffn2_w2)
        new_k_sample = k_all[:, T:]
        new_v_sample = v_all[:, T:]
        new_conv_sample = c[:, T:]
    y_sample = xs

    return (y_prompt, y_sample, new_k_prompt, new_v_prompt, new_conv_prompt,
            new_k_sample, new_v_sample, new_conv_sample)
```

```python
import contextlib
import numpy as np
import ml_dtypes
import concourse.bass as bass
import concourse.mybir as mybir
from concourse.bass_utils import run_bass_kernel_spmd

F32 = mybir.dt.float32
BF16 = mybir.dt.bfloat16
ALU = mybir.AluOpType
AF = mybir.ActivationFunctionType
AX = mybir.AxisListType

COMPUTE = ("pe", "act", "dve", "pool")
NEG = -1e30
EPS = 1e-6
D = 1024
DFF = 2816
NCH = 22
NS = 6
GS = 3


class Op:
    __slots__ = ("idx", "stream", "fn", "reads", "writes", "is_dma", "deps", "flag", "ts", "sem")

    def __init__(self, idx, stream, fn, reads, writes, is_dma):
        self.idx = idx
        self.stream = stream
        self.fn = fn
        self.reads = reads
        self.writes = writes
        self.is_dma = is_dma
        self.deps = set()
        self.flag = False
        self.ts = None
        self.sem = None


_CUR = {}
ATTACH_WAITS = False
DEBUG_OUT = False
MAXT = 4
STAGES = 99
SUBCUT = 99
DEBUG_NAMES = ()


def ST(ins, *keys):
    if not ATTACH_WAITS:
        return ins
    op = _CUR["op"]; ops = _CUR["ops"]
    best = {}
    for k in keys:
        cand = [ops[d] for d in op.deps if k in ops[d].writes]
        if not cand:
            continue
        p = max(cand, key=lambda o: o.idx)
        if p.sem is None or p.stream == "pe" and not p.is_dma:
            continue
        sid = id(p.sem)
        if sid not in best or best[sid][1] < p.ts:
            best[sid] = (p.sem, p.ts)
    assert len(best) <= 1, (keys, best)
    for sem, ts in best.values():
        ins._wait_ge(sem, ts)
    return ins


def _same_eng_hazard(p, op):
    pw = set(p.writes)
    return bool(pw & set(op.reads)) or bool(pw & set(op.writes)) or bool(set(p.reads) & set(op.writes))


class Prog:
    def __init__(self, nc, n_dma_sems=20):
        self.nc = nc
        self.ops = []
        self.last_w = {}
        self.readers = {}
        self.n_dma_sems = n_dma_sems

    def _add(self, stream, fn, reads, writes, is_dma):
        op = Op(len(self.ops), stream, fn, tuple(reads), tuple(writes), is_dma)
        for k in op.reads:
            w = self.last_w.get(k)
            if w is not None:
                op.deps.add(w)
        for k in op.writes:
            w = self.last_w.get(k)
            if w is not None:
                op.deps.add(w)
            for r in self.readers.get(k, ()):
                op.deps.add(r)
        for k in op.reads:
            self.readers.setdefault(k, []).append(op.idx)
        for k in op.writes:
            self.last_w[k] = op.idx
            self.readers[k] = []
        op.deps.discard(op.idx)
        self.ops.append(op)
        return op

    def pe(self, fn, reads=(), writes=()):
        return self._add("pe", fn, reads, writes, False)

    def act(self, fn, reads=(), writes=()):
        return self._add("act", fn, reads, writes, False)

    def dve(self, fn, reads=(), writes=()):
        return self._add("dve", fn, reads, writes, False)

    def pool(self, fn, reads=(), writes=()):
        return self._add("pool", fn, reads, writes, False)

    def dma(self, queue, fn, reads=(), writes=()):
        return self._add(queue, fn, reads, writes, True)

    def emit(self, es):
        nc = self.nc
        ops = self.ops
        streams = ["pe", "act", "dve", "pool", "sync"]
        for op in ops:
            for d in op.deps:
                p = ops[d]
                if p.is_dma:
                    continue
                if p.stream != op.stream or op.is_dma:
                    p.flag = True
                elif p.stream in ("act", "dve", "pool"):
                    if _same_eng_hazard(p, op):
                        p.flag = True
        csem = {s: es.enter_context(nc.semaphore("c_" + s)) for s in COMPUTE}
        dsems = {}
        nds = {"sync": 12, "pool": 14}
        for q in ("sync", "pool"):
            dsems[q] = [es.enter_context(nc.semaphore("d_%s_%d" % (q, i))) for i in range(nds[q])]
        ccount = {s: 0 for s in COMPUTE}
        dcount = {q: [0] * nds[q] for q in dsems}
        drr = {q: 0 for q in dsems}
        dprev = {}
        for op in ops:
            if op.is_dma:
                q = op.stream
                i = drr[q]
                drr[q] = (i + 1) % nds[q]
                prev = dcount[q][i]
                dcount[q][i] = prev + 16
                op.sem = dsems[q][i]
                op.ts = prev + 16
                op.flag = True
                dprev[op.idx] = prev
            elif op.flag:
                ccount[op.stream] += 1
                op.sem = csem[op.stream]
                op.ts = ccount[op.stream]
        by_stream = {s: [o for o in ops if o.stream == s] for s in streams}
        block = es.enter_context(nc.Block())

        def make(stream):
            def body(eng):
                seen = {}
                for op in by_stream[stream]:
                    waits = {}
                    for d in op.deps:
                        p = ops[d]
                        if not p.flag:
                            continue
                        if (not p.is_dma) and p.stream == stream and not op.is_dma:
                            if stream == "pe":
                                continue
                            if not _same_eng_hazard(p, op):
                                continue
                        key = id(p.sem)
                        if waits.get(key, (None, 0))[1] < p.ts:
                            waits[key] = (p.sem, p.ts)
                    if op.is_dma and dprev[op.idx] > 0:
                        key = id(op.sem)
                        if waits.get(key, (None, 0))[1] < dprev[op.idx]:
                            waits[key] = (op.sem, dprev[op.idx])
                    for key, (sem, ts) in waits.items():
                        if seen.get(key, 0) >= ts:
                            continue
                        eng.wait_ge(sem, ts)
                        seen[key] = ts
                    _CUR["op"] = op; _CUR["ops"] = ops
                    ins = op.fn(eng)
                    if op.is_dma:
                        ins.then_inc(op.sem, 16)
                    elif op.flag:
                        ins.then_inc(op.sem, 1)
                if stream == "sync":
                    for q in dsems:
                        for i, sem in enumerate(dsems[q]):
                            if dcount[q][i] > 0:
                                eng.wait_ge(sem, dcount[q][i])
            return body

        block.tensor(make("pe"))
        block.scalar(make("act"))
        block.vector(make("dve"))
        block.gpsimd(make("pool"))
        block.sync(make("sync"))


def _tiles():
    tiles = []
    for t in range(4):
        subs = []
        off = 128 if t == 0 else 0
        if t == 0:
            subs.append(dict(kind="halo", c0=0, n=128, h=4, blk=-1))
        for i in range(4):
            subs.append(dict(kind="main", c0=off + 128 * i, n=128, h=i, blk=4 * t + i))
        if t == 3:
            subs.append(dict(kind="samp", c0=512, n=16, h=4, blk=-1))
        main_cg = (off, 512)
        cgs_all = [main_cg]
        cgs_ms = [main_cg]
        if t == 0:
            cgs_all.append((0, 128))
        if t == 3:
            cgs_all.append((512, 16))
            cgs_ms.append((512, 16))
        tiles.append(dict(t=t, subs=subs, main_cg=main_cg, cgs_all=cgs_all, cgs_ms=cgs_ms, off=off))
    return tiles


def build_nc():
    nc = bass.Bass("TRN2", target_bir_lowering=False)

    def din(name, shape, dt=F32):
        return nc.dram_tensor(name, list(shape), dt, kind="ExternalInput").ap()

    def dout(name, shape):
        return nc.dram_tensor(name, list(shape), F32, kind="ExternalOutput").ap()

    xh = din("xh", [128, D]); xm = din("xm", [2048, D]); xs = din("xs", [16, D])
    ck = din("ck", [16, 128, 128]); cv = din("cv", [16, 128, 128]); stc = din("stc", [16, 30, 512])
    hmask_d = din("hmask", [128, 1])
    g1 = din("ffn1_norm", [D]); w11 = din("ffn1_w1", [D, DFF]); w13 = din("ffn1_w3", [D, DFF]); w12 = din("ffn1_w2", [DFF, D])
    gm = din("mix_norm", [D]); w_in = din("w_in", [D, 3840]); qn_d = din("q_norm", [64]); kn_d = din("k_norm", [64])
    relb = din("rel_bias", [32, 8]); sinks_d = din("sinks", [8]); w_ao = din("w_attn_out", [512, D])
    wdw = din("w_dw", [31, 512]); bdw = din("b_dw", [512]); lng = din("conv_ln_g", [512]); lnb = din("conv_ln_b", [512])
    w_co = din("w_conv_out", [512, D]); w_o = din("w_out", [D, D])
    g2 = din("ffn2_norm", [D]); w21 = din("ffn2_w1", [D, DFF]); w23 = din("ffn2_w3", [D, DFF]); w22 = din("ffn2_w2", [DFF, D])
    identb_d = din("identb", [128, 128], BF16); identf_d = din("identf", [128, 128]); er_d = din("er", [33, 383])
    sel_d = din("sel", [120, 4])

    y_m = dout("y_m", [2048, D]); y_s = dout("y_s", [16, D])
    nkp = dout("nkp", [128, 128]); nvp = dout("nvp", [128, 128]); ncp = dout("ncp", [30, 512])
    nks = dout("nks", [16, 128, 128]); nvs = dout("nvs", [16, 128, 128]); ncs = dout("ncs", [16, 30, 512])

    if DEBUG_OUT:
        dbg1 = dout("dbg1", [2048, D]); dbg2 = dout("dbg2", [2048, D]); dbg3 = dout("dbg3", [2048, D])

    gscr = nc.dram_tensor("gscr", [8, 384], F32, kind="Internal").ap()

    es = contextlib.ExitStack()
    with es:
        def sb(name, shape, dt=F32):
            return es.enter_context(nc.sbuf_tensor(name, list(shape), dt))

        H = sb("H", [128, 5, D])
        UT = sb("UT", [128, 8, 640], BF16)
        ring = sb("ring", [128, NS, 3072], BF16)
        actb = [sb("act%d" % i, [128, GS, 640], BF16) for i in range(2)]
        stmp = [sb("stmp%d" % i, [128, 512]) for i in range(2)]
        xn = sb("xn", [128, D], BF16)
        junk = sb("junk", [128, D], BF16)
        biasT = sb("biasT", [128, 2, 8, 128])
        identb = sb("identb_s", [128, 128], BF16)
        identf = sb("identf_s", [128, 128])
        onesb = sb("onesb", [128, 64], BF16)
        onesf = sb("onesf", [128, 128])
        gT = sb("gT", [128, 3, 8])
        epsc = sb("epsc", [128, 1])
        zcol = sb("zcol", [128, 1])
        hmask = sb("hmask_s", [128, 1])
        ssq = sb("ssq", [128, 8])
        rstd = sb("rstd", [128, 8])
        qg = sb("qg", [128, 8, 64])
        kg = sb("kg", [128, 2, 64])
        esink = sb("esink", [128, 4])
        wdwT = sb("wdwT", [128, 4, 31])
        cvec = sb("cvec", [128, 3, 4])
        er = sb("er_s", [33, 383])
        rbx = sb("rbx", [33, 8])
        kT = sb("kT", [128, 768], BF16)
        Vb = sb("Vb", [128, 6, 128], BF16)
        gluT = sb("gluT", [128, 4, 670], BF16)
        glu32 = sb("glu32", [128, 4, 46])
        qkv = sb("qkv", [128, 768])
        sq = sb("sq", [128, 640])
        ssh = sb("ssh", [128, 10])
        rsh = sb("rsh", [128, 10])
        qtmp = sb("qtmp", [128, 512])
        qb = sb("qb", [128, 512], BF16)
        kf = sb("kf", [128, 128])
        kb = sb("kb", [128, 128], BF16)
        qT = sb("qT", [128, 4, 640], BF16)
        aT = sb("aT", [128, 4, 640], BF16)
        sc = sb("sc", [128, 1024])
        PT = sb("PT", [128, 2, 1024], BF16)
        dtmp = sb("dtmp", [128, 512])
        diags = [sb("diag%d" % i, [128, 31, 128], BF16) for i in range(2)]
        yb = sb("yb", [128, 4, 528])
        ysq = [sb("ysq%d" % i, [128, 512]) for i in range(2)]
        mean = sb("mean", [128, 528])
        var = sb("var", [128, 528])
        sw = sb("sw", [128, 4, 528], BF16)
        gA = sb("gA", [128, 512]); gC = sb("gC", [128, 512])
        mT = sb("mT", [128, 8, 528], BF16)
        Ks = sb("Ks", [128, 16, 128], BF16)
        Vs = sb("Vs", [128, 16, 128], BF16)
        KTs = sb("KTs", [128, 16, 128], BF16)
        st = sb("st", [120, 512])
        wrep = sb("wrep", [120, 512])
        sel = sb("sel_s", [120, 4])
        pst = sb("pst", [8, 6, 128])
        wdw_s = sb("wdw_s", [31, 512])
        otok = sb("otok", [128, 512])
        gs = sb("gs", [8, 384])
        ps = [es.enter_context(nc.psum_tensor("ps%d" % i, [128, 512], F32)) for i in range(8)]

        P = Prog(nc)
        bank_rr = [0]

        def bank():
            i = bank_rr[0]
            bank_rr[0] = (i + 1) % 8
            return i

        pieces = []

        def ffn_pieces(w1, w3, w2):
            w1v = w1.rearrange("(k p) n -> p k n", p=128)
            w3v = w3.rearrange("(k p) n -> p k n", p=128)
            out = []
            for c in range(NCH):
                out.append([
                    (lambda s, c=c: ring[:, s, 0:1024].rearrange("p (k n) -> p k n", k=8), w1v[:, :, c * 128:(c + 1) * 128]),
                    (lambda s, c=c: ring[:, s, 1024:2048].rearrange("p (k n) -> p k n", k=8), w3v[:, :, c * 128:(c + 1) * 128]),
                    (lambda s, c=c: ring[:, s, 2048:3072], w2[c * 128:(c + 1) * 128, :]),
                ])
            return out

        w_inv = w_in.rearrange("(k p) n -> p k n", p=128)
        w_aov = [w_ao[0:256, :].rearrange("(j p) n -> p j n", p=64), w_ao[256:512, :].rearrange("(j p) n -> p j n", p=64)]
        w_cov = w_co.rearrange("(k p) n -> p k n", p=128)
        w_ov = w_o.rearrange("(k p) n -> p k n", p=128)

        def mixer_pieces():
            out = []
            for q in range(2):
                out.append([(lambda s: ring[:, s, :].rearrange("p (k n) -> p k n", k=8), w_inv[:, :, q * 384:(q + 1) * 384])])
            for c in range(4):
                out.append([
                    (lambda s: ring[:, s, 0:1024].rearrange("p (k n) -> p k n", k=8), w_inv[:, :, 768 + c * 128:768 + (c + 1) * 128]),
                    (lambda s: ring[:, s, 1024:2048].rearrange("p (k n) -> p k n", k=8), w_inv[:, :, 1280 + c * 128:1280 + (c + 1) * 128]),
                ])
            for i in range(8):
                out.append([
                    (lambda s: ring[:, s, 0:1024].rearrange("p (k n) -> p k n", k=8), w_inv[:, :, 1792 + i * 128:1792 + (i + 1) * 128]),
                    (lambda s: ring[:, s, 1024:2048].rearrange("p (k n) -> p k n", k=8), w_inv[:, :, 2816 + i * 128:2816 + (i + 1) * 128]),
                    (lambda s: ring[0:64, s, 2048:2560].rearrange("p (j n) -> p j n", j=4), w_aov[0][:, :, i * 128:(i + 1) * 128]),
                    (lambda s: ring[64:128, s, 2048:2560].rearrange("p (j n) -> p j n", j=4), w_aov[1][:, :, i * 128:(i + 1) * 128]),
                    (lambda s: ring[:, s, 2560:3072].rearrange("p (k n) -> p k n", k=4), w_cov[:, :, i * 128:(i + 1) * 128]),
                ])
            for (a, b) in ((0, 384), (384, 768), (768, 1024)):
                out.append([(lambda s, a=a, b=b: ring[:, s, 0:8 * (b - a)].rearrange("p (k n) -> p k n", k=8), w_ov[:, :, a:b])])
            return out

        for t in range(4):
            pieces += ffn_pieces(w11, w13, w12)
            pieces += mixer_pieces()
            pieces += ffn_pieces(w21, w23, w22)
        n_pieces = len(pieces)
        loaded = [0]

        def w_load(k):
            if k >= n_pieces:
                return
            assert k == loaded[0]
            s = k % NS
            for j, (dstf, src) in enumerate(pieces[k]):
                dst = dstf(s)
                P.dma("pool", lambda e, dst=dst, src=src: e.dma_start(out=dst, in_=src), writes=[("ring", s, j)])
            loaded[0] = k + 1

        cur = [0]

        def RK(s):
            return [("ring", s, j) for j in range(5)]

        def w_use():
            k = cur[0]
            assert k < loaded[0], "piece not loaded"
            cur[0] += 1
            return k, k % NS

        def w_release(k):
            w_load(k + NS)

        def ld(dst, src, key, q="sync"):
            P.dma(q, lambda e: e.dma_start(out=dst, in_=src), writes=[key])

        with nc.allow_non_contiguous_dma(reason="tiny one-time parameter loads"):
            for k in range(NS):
                w_load(k)
            ld(identb[:], identb_d, "identb"); ld(identf[:], identf_d, "identf")
            ld(er[:], er_d, "er"); ld(sel[:], sel_d, "sel"); ld(hmask[:], hmask_d, "hmask")
            ld(pst[0:8, 0, :], g1.rearrange("(k p) -> k p", p=128), ("pst", 0))
            ld(pst[0:8, 1, :], gm.rearrange("(k p) -> k p", p=128), ("pst", 1))
            ld(pst[0:8, 2, :], g2.rearrange("(k p) -> k p", p=128), ("pst", 2))
            ld(pst[0:4, 3, :], bdw.rearrange("(k p) -> k p", p=128), ("pst", 3))
            ld(pst[0:4, 4, :], lng.rearrange("(k p) -> k p", p=128), ("pst", 4))
            ld(pst[0:4, 5, :], lnb.rearrange("(k p) -> k p", p=128), ("pst", 5))
            ld(wdw_s[:, :], wdw, "wdw_s")
            ld(qg[:, 0, :], qn_d.partition_broadcast(128), "qg0")
            ld(kg[:, 0, :], kn_d.partition_broadcast(128), "kg0")
            ld(esink[0:64, :], sinks_d[0:4].partition_broadcast(64), "esink")
            ld(esink[64:128, :], sinks_d[4:8].partition_broadcast(64), "esink2")
            ld(rbx[0:32, :], relb, "rbx")
            for a in range(4):
                ld(wrep[a * 30:(a + 1) * 30, :], wdw[0:30, :], ("wrep", a))
        def tsmall(dst, src, npart, rkeys, wkey):
            b = bank()
            P.pe(lambda e, b=b: ST(e.transpose(ps[b][:, 0:npart], src, identf[0:npart, 0:npart]), *rkeys), reads=rkeys + ["identf"], writes=[("ps", b)])
            P.dve(lambda e, b=b: e.tensor_copy(out=dst, in_=ps[b][:, 0:npart]), reads=[("ps", b)], writes=[wkey])
        for i in range(3):
            tsmall(gT[:, i, :], pst[0:8, i, :], 8, [("pst", i)], "gT%d" % i)
            tsmall(cvec[:, i, :], pst[0:4, 3 + i, :], 4, [("pst", 3 + i)], "cv%d" % i)
        for c in range(4):
            tsmall(wdwT[:, c, :], wdw_s[0:31, c * 128:(c + 1) * 128], 31, ["wdw_s"], "wdwT")
        P.dve(lambda e: e.memset(onesb[:], 1.0), writes=["onesb"])
        P.dve(lambda e: e.memset(onesf[:], 1.0), writes=["onesf"])
        P.dve(lambda e: e.memset(epsc[:], EPS), writes=["epsc"])
        P.dve(lambda e: e.memset(zcol[:], 0.0), writes=["zcol"])
        P.dve(lambda e: e.memset(rbx[32:33, :], NEG), writes=["rbx2"])
        P.dve(lambda e: e.memset(gluT[:, :, 0:30], 0.0), writes=[("gluc", c) for c in range(4)])
        P.dve(lambda e: e.tensor_scalar(out=qg[:, 0, :], in0=qg[:, 0, :], scalar1=0.125, scalar2=None, op0=ALU.mult),
              reads=["qg0"], writes=["qg0"])
        for hh in range(1, 8):
            P.dve(lambda e, hh=hh: e.tensor_copy(out=qg[:, hh, :], in_=qg[:, 0, :]), reads=["qg0"], writes=[("qg", hh)])
        P.dve(lambda e: e.tensor_copy(out=kg[:, 1, :], in_=kg[:, 0, :]), reads=["kg0"], writes=["kg1"])
        QG = ["qg0"] + [("qg", hh) for hh in range(1, 8)]
        P.act(lambda e: e.activation(out=esink[:], in_=esink[:], func=AF.Exp), reads=["esink", "esink2"], writes=["esink", "esink2"])
        def build_bias():
            b = bank()
            P.pe(lambda e, b=b: e.matmul(ps[b][0:8, 0:383], lhsT=rbx[:, :], rhs=er[:, 0:383], start=True, stop=True),
                 reads=["er", "rbx", "rbx2"], writes=[("ps", b)])
            P.dve(lambda e, b=b: e.tensor_copy(out=gs[:, 0:383], in_=ps[b][0:8, 0:383]), reads=[("ps", b)], writes=["gs"])
            P.dma("sync", lambda e: e.dma_start(out=gscr[:, 0:383], in_=gs[:, 0:383]), reads=["gs"], writes=["gscr"])
            for j in range(128):
                for c in range(2):
                    off = 127 - j + 128 * (1 - c)
                    P.dma("sync", lambda e, j=j, c=c, off=off: e.dma_start(out=biasT[j:j + 1, c, :, :], in_=gscr[:, off:off + 128].unsqueeze(0)),
                          reads=["gscr"], writes=[("biasT", c, j)])
        BIAS = [("biasT", c, j) for c in range(2) for j in range(128)]
        build_bias()

        tiles = _tiles()
        if DEBUG_OUT:
            P.dve(lambda e: e.memset(kT[:, :], 0.0), writes=[("kT", i) for i in range(6)])
            P.dve(lambda e: e.memset(Vb[:, :, :], 0.0), writes=[("V", i) for i in range(6)])
            P.dve(lambda e: e.memset(qT[:, :, :], 0.0), writes=[("qT", c) for c in (0, 128, 256, 384, 512)])
            P.dve(lambda e: e.memset(aT[:, :, :], 0.0), writes=[("aT", c) for c in (0, 128, 256, 384, 512)])
            P.dve(lambda e: e.memset(gluT[:, :, :], 0.0), writes=[("glu", c) for c in range(4)] + [("gluc", c) for c in range(4)])
            P.dve(lambda e: e.memset(sw[:, :, :], 0.0), writes=[("sw", c, o) for c in range(4) for o in (0, 512)])
            P.dve(lambda e: e.memset(mT[:, :, :], 0.0), writes=[("mT", i, o) for i in range(8) for o in (0, 512)])

        def norm_stage(T, gi, subs):
            n = len(subs)
            P.dve(lambda e: e.memset(ssq[:, :], 0.0), writes=[("ssq", i) for i in range(8)])
            for i, s in enumerate(subs):
                P.act(lambda e, s=s, i=i: e.activation(out=junk[0:s["n"], :], in_=H[0:s["n"], s["h"], :], func=AF.Square,
                                                       accum_out=ssq[0:s["n"], i:i + 1]),
                      reads=[("H", s["h"])], writes=["junk", ("ssq", i)])
            P.act(lambda e: e.activation(out=rstd[:, 0:n], in_=ssq[:, 0:n], func=AF.Sqrt, bias=epsc[:, 0:1], scale=1.0 / D),
                  reads=[("ssq", i) for i in range(n)] + ["epsc"], writes=["rstd"])
            P.dve(lambda e: e.reciprocal(out=rstd[:, 0:n], in_=rstd[:, 0:n]), reads=["rstd"], writes=["rstd"])
            for i, s in enumerate(subs):
                nt = s["n"]
                P.act(lambda e, s=s, i=i, nt=nt: e.activation(out=xn[0:nt, :], in_=H[0:nt, s["h"], :], func=AF.Copy,
                                                              scale=rstd[0:nt, i:i + 1]),
                      reads=[("H", s["h"]), "rstd"], writes=["xn"])
                b = bank()
                pb = ps[b][:, :].bitcast(BF16)

                def fn(e, nt=nt, pb=pb):
                    ins = None
                    for k in range(8):
                        ins = ST(e.transpose(pb[:, k * 128:k * 128 + nt], xn[0:nt, k * 128:(k + 1) * 128], identb[0:nt, 0:nt]), "xn")
                    return ins
                P.pe(fn, reads=["xn", "identb"], writes=[("ps", b)])
                P.dve(lambda e, s=s, nt=nt, pb=pb: e.tensor_tensor(
                    out=UT[:, :, s["c0"]:s["c0"] + nt],
                    in0=pb[:, 0:1024].rearrange("p (k n) -> p k n", k=8)[:, :, 0:nt],
                    in1=gT[:, gi, :].unsqueeze(2).to_broadcast([128, 8, nt]), op=ALU.mult),
                    reads=[("ps", b), "gT%d" % gi], writes=[("UT", s["c0"])])

        def ut_keys(T, cg):
            return [("UT", s["c0"]) for s in T["subs"] if cg[0] <= s["c0"] < cg[0] + cg[1]]

        def ffn_stage(T, cgs, subs):
            groups = [list(range(g, min(g + GS, NCH))) for g in range(0, NCH, GS)]
            for gi_, grp in enumerate(groups):
                ab = actb[gi_ % 2]
                used = []
                for ci, c in enumerate(grp):
                    k, s = w_use()
                    used.append((k, s))
                    w1c = ring[:, s, 0:1024].rearrange("p (k n) -> p k n", k=8)
                    w3c = ring[:, s, 1024:2048].rearrange("p (k n) -> p k n", k=8)
                    for cg in cgs:
                        c0, n = cg
                        b1 = bank(); b3 = bank()

                        def fn(e, b1=b1, b3=b3, c0=c0, n=n, w1c=w1c, w3c=w3c, s=s):
                            ins = None
                            for kk in range(8):
                                ins = ST(e.matmul(ps[b1][:, 0:n], lhsT=w1c[:, kk, :], rhs=UT[:, kk, c0:c0 + n], start=(kk == 0), stop=(kk == 7)), ("ring", s, 0))
                            for kk in range(8):
                                ins = ST(e.matmul(ps[b3][:, 0:n], lhsT=w3c[:, kk, :], rhs=UT[:, kk, c0:c0 + n], start=(kk == 0), stop=(kk == 7)), ("ring", s, 1))
                            return ins
                        P.pe(fn, reads=RK(s) + ut_keys(T, cg), writes=[("ps", b1), ("ps", b3)])
                        tb = stmp[(ci + (0 if cg is cgs[0] else 1)) % 2]
                        tk = "stmp%d" % ((ci + (0 if cg is cgs[0] else 1)) % 2)
                        P.act(lambda e, b1=b1, n=n, tb=tb: e.activation(out=tb[:, 0:n], in_=ps[b1][:, 0:n], func=AF.Silu),
                              reads=[("ps", b1)], writes=[tk])
                        P.dve(lambda e, b3=b3, n=n, tb=tb, ab=ab, ci=ci, c0=c0: e.tensor_tensor(
                            out=ab[:, ci, c0:c0 + n], in0=tb[:, 0:n], in1=ps[b3][:, 0:n], op=ALU.mult),
                            reads=[tk, ("ps", b3)], writes=[("act", gi_ % 2, ci, c0)])
                for sidx, sub in enumerate(subs):
                    nt = sub["n"]
                    cgk = [cg for cg in cgs if cg[0] <= sub["c0"] < cg[0] + cg[1]][0]
                    for half in range(2):
                        b = bank()

                        def fn(e, b=b, nt=nt, sub=sub, half=half, used=used, ab=ab, gi_=gi_, cgk=cgk):
                            ins = None
                            for ci, (k, s) in enumerate(used):
                                ins = ST(e.matmul(ps[b][0:nt, :], lhsT=ab[:, ci, sub["c0"]:sub["c0"] + nt],
                                                  rhs=ring[:, s, 2048 + half * 512:2048 + (half + 1) * 512],
                                                  start=(ci == 0), stop=(ci == len(used) - 1)), ("act", gi_ % 2, ci, cgk[0]))
                            return ins
                        P.pe(fn, reads=[kk_ for (k, s) in used for kk_ in RK(s)] + [("act", gi_ % 2, ci, cgk[0]) for ci in range(len(used))],
                             writes=[("ps", b)])
                        P.dve(lambda e, b=b, nt=nt, sub=sub, half=half: e.scalar_tensor_tensor(
                            out=H[0:nt, sub["h"], half * 512:(half + 1) * 512], in0=ps[b][0:nt, :], scalar=0.5,
                            in1=H[0:nt, sub["h"], half * 512:(half + 1) * 512], op0=ALU.mult, op1=ALU.add),
                            reads=[("ps", b), ("H", sub["h"])], writes=[("H", sub["h"])])
                for (k, s) in used:
                    w_release(k)

        def qkv_stage(T):
            k0, s0 = w_use()
            k1, s1 = w_use()
            wa = ring[:, s0, :].rearrange("p (k n) -> p k n", k=8)
            wb = ring[:, s1, :].rearrange("p (k n) -> p k n", k=8)
            last = T["t"] == 3
            for sub in T["subs"]:
                nt = sub["n"]; c0 = sub["c0"]
                ba = bank(); bb = bank()

                def fn(e, ba=ba, bb=bb, nt=nt, c0=c0):
                    ins = None
                    for kk in range(8):
                        ins = ST(e.matmul(ps[ba][0:nt, 0:384], lhsT=UT[:, kk, c0:c0 + nt], rhs=wa[:, kk, :], start=(kk == 0), stop=(kk == 7)), ("UT", c0))
                    for kk in range(8):
                        ins = ST(e.matmul(ps[bb][0:nt, 0:384], lhsT=UT[:, kk, c0:c0 + nt], rhs=wb[:, kk, :], start=(kk == 0), stop=(kk == 7)), ("UT", c0))
                    return ins
                P.pe(fn, reads=RK(s0) + RK(s1) + [("UT", c0)], writes=[("ps", ba), ("ps", bb)])
                P.act(lambda e, ba=ba, nt=nt: e.activation(out=qkv[0:nt, 0:384], in_=ps[ba][0:nt, 0:384], func=AF.Copy),
                      reads=[("ps", ba)], writes=["qkvA"])
                P.dve(lambda e, bb=bb, nt=nt: e.tensor_copy(out=qkv[0:nt, 384:768], in_=ps[bb][0:nt, 0:384]),
                      reads=[("ps", bb)], writes=["qkvB"])
                P.act(lambda e, nt=nt: e.activation(out=sq[0:nt, :], in_=qkv[0:nt, 0:640], func=AF.Square),
                      reads=["qkvA", "qkvB"], writes=["sq"])
                P.dve(lambda e, nt=nt: e.tensor_reduce(out=ssh[0:nt, :], in_=sq[0:nt, :].rearrange("p (h d) -> p h d", d=64),
                                                       axis=AX.X, op=ALU.add), reads=["sq"], writes=["ssh"])
                P.act(lambda e, nt=nt: e.activation(out=rsh[0:nt, :], in_=ssh[0:nt, :], func=AF.Sqrt, bias=epsc[0:nt, 0:1], scale=1.0 / 64),
                      reads=["ssh", "epsc"], writes=["rsh"])
                P.dve(lambda e, nt=nt: e.reciprocal(out=rsh[0:nt, :], in_=rsh[0:nt, :]), reads=["rsh"], writes=["rsh"])
                P.dve(lambda e, nt=nt: e.tensor_tensor(
                    out=qtmp[0:nt, :].rearrange("p (h d) -> p h d", d=64), in0=qkv[0:nt, 0:512].rearrange("p (h d) -> p h d", d=64),
                    in1=rsh[0:nt, 0:8].unsqueeze(2).to_broadcast([nt, 8, 64]), op=ALU.mult),
                    reads=["qkvA", "qkvB", "rsh"], writes=["qtmp"])
                P.dve(lambda e, nt=nt: e.tensor_tensor(
                    out=qb[0:nt, :].rearrange("p (j l d) -> p l j d", j=4, l=2),
                    in0=qtmp[0:nt, :].rearrange("p (l j d) -> p l j d", l=2, j=4),
                    in1=qg[0:nt, :, :].rearrange("p (l j) d -> p l j d", l=2), op=ALU.mult),
                    reads=["qtmp"] + QG, writes=["qb"])
                P.dve(lambda e, nt=nt: e.tensor_tensor(
                    out=kf[0:nt, :].rearrange("p (h d) -> p h d", d=64), in0=qkv[0:nt, 512:640].rearrange("p (h d) -> p h d", d=64),
                    in1=rsh[0:nt, 8:10].unsqueeze(2).to_broadcast([nt, 2, 64]), op=ALU.mult),
                    reads=["qkvB", "rsh"], writes=["kf"])
                P.dve(lambda e, nt=nt: e.tensor_tensor(
                    out=kf[0:nt, :].rearrange("p (h d) -> p h d", d=64), in0=kf[0:nt, :].rearrange("p (h d) -> p h d", d=64),
                    in1=kg[0:nt, :, :], op=ALU.mult), reads=["kf", "kg0", "kg1"], writes=["kf"])
                P.act(lambda e, nt=nt: e.activation(out=kb[0:nt, :], in_=kf[0:nt, :], func=AF.Copy), reads=["kf"], writes=["kb"])
                vslot = 1 + c0 // 128
                if sub["kind"] != "samp":
                    P.act(lambda e, vslot=vslot: e.activation(out=Vb[:, vslot, :], in_=qkv[:, 640:768], func=AF.Copy),
                          reads=["qkvB"], writes=[("V", vslot)])
                bq = bank()
                pq = ps[bq][:, :].bitcast(BF16)

                def fn(e, nt=nt, pq=pq):
                    ins = None
                    for j in range(4):
                        ins = ST(e.transpose(pq[:, j * 128:j * 128 + nt], qb[0:nt, j * 128:(j + 1) * 128], identb[0:nt, 0:nt]), "qb")
                    ins = ST(e.transpose(pq[:, 512:512 + nt], kb[0:nt, :], identb[0:nt, 0:nt]), "kb")
                    return ins
                P.pe(fn, reads=["qb", "kb", "identb"], writes=[("ps", bq)])
                P.act(lambda e, nt=nt, c0=c0, pq=pq: e.activation(
                    out=qT[:, :, c0:c0 + nt], in_=pq[:, 0:512].rearrange("p (j n) -> p j n", j=4)[:, :, 0:nt], func=AF.Copy),
                    reads=[("ps", bq)], writes=[("qT", c0)])
                if sub["kind"] != "samp":
                    P.act(lambda e, nt=nt, c0=c0, pq=pq: e.activation(out=kT[:, 128 + c0:128 + c0 + nt], in_=pq[:, 512:512 + nt], func=AF.Copy),
                          reads=[("ps", bq)], writes=[("kT", vslot)])
                if last and sub["kind"] == "main" and sub["blk"] == 15:
                    P.dma("sync", lambda e: e.dma_start(out=nkp, in_=kf[:, :]), reads=["kf"], writes=["o_nkp"])
                    P.dma("sync", lambda e: e.dma_start(out=nvp, in_=qkv[:, 640:768]), reads=["qkvB"], writes=["o_nvp"])
                if sub["kind"] == "samp":
                    P.dma("sync", lambda e: e.dma_start(out=nks[:, 127, :], in_=kf[0:16, :]), reads=["kf"], writes=["o_nks_new"])
                    P.dma("sync", lambda e: e.dma_start(out=nvs[:, 127, :], in_=qkv[0:16, 640:768]), reads=["qkvB"], writes=["o_nvs_new"])
            w_release(k0); w_release(k1)

        def glu_stage(T):
            last = T["t"] == 3
            for c in range(4):
                k, s = w_use()
                wa = ring[:, s, 0:1024].rearrange("p (k n) -> p k n", k=8)
                wb = ring[:, s, 1024:2048].rearrange("p (k n) -> p k n", k=8)
                for cg in T["cgs_all"]:
                    c0, n = cg
                    ba = bank(); bb = bank()

                    def fn(e, ba=ba, bb=bb, c0=c0, n=n, wa=wa, wb=wb, s=s):
                        ins = None
                        for kk in range(8):
                            ins = ST(e.matmul(ps[ba][:, 0:n], lhsT=wa[:, kk, :], rhs=UT[:, kk, c0:c0 + n], start=(kk == 0), stop=(kk == 7)), ("ring", s, 0))
                        for kk in range(8):
                            ins = ST(e.matmul(ps[bb][:, 0:n], lhsT=wb[:, kk, :], rhs=UT[:, kk, c0:c0 + n], start=(kk == 0), stop=(kk == 7)), ("ring", s, 1))
                        return ins
                    P.pe(fn, reads=RK(s) + ut_keys(T, cg), writes=[("ps", ba), ("ps", bb)])
                    P.act(lambda e, bb=bb, n=n: e.activation(out=gA[:, 0:n], in_=ps[bb][:, 0:n], func=AF.Sigmoid),
                          reads=[("ps", bb)], writes=["gA"])
                    P.dve(lambda e, ba=ba, n=n, c0=c0, c=c: e.tensor_tensor(
                        out=gluT[:, c, 30 + c0:30 + c0 + n], in0=gA[:, 0:n], in1=ps[ba][:, 0:n], op=ALU.mult),
                        reads=["gA", ("ps", ba)], writes=[("glu", c)])
                    if last:
                        if n == 512:
                            P.dve(lambda e, ba=ba, c=c: e.tensor_tensor(
                                out=glu32[:, c, 0:30], in0=gA[:, 482:512], in1=ps[ba][:, 482:512], op=ALU.mult),
                                reads=["gA", ("ps", ba)], writes=[("glu32m", c)])
                        else:
                            P.dve(lambda e, ba=ba, c=c: e.tensor_tensor(
                                out=glu32[:, c, 30:46], in0=gA[:, 0:16], in1=ps[ba][:, 0:16], op=ALU.mult),
                                reads=["gA", ("ps", ba)], writes=[("glu32s", c)])
                w_release(k)

        def attn_stage(T):
            first_core_blk = True
            for sub in T["subs"]:
                if sub["kind"] != "main":
                    continue
                c0 = sub["c0"]
                pslot = c0 // 128
                oslot = 1 + c0 // 128
                use_mask = (sub["blk"] == 0)
                for c, slot in enumerate((pslot, oslot)):
                    b0 = bank(); b1 = bank()

                    def fn(e, b0=b0, b1=b1, slot=slot, c0=c0):
                        ST(e.matmul(ps[b0][:, :], lhsT=kT[0:64, slot * 128:(slot + 1) * 128], rhs=qT[0:64, :, c0:c0 + 128], start=True, stop=True), ("kT", slot))
                        return ST(e.matmul(ps[b1][:, :], lhsT=kT[64:128, slot * 128:(slot + 1) * 128], rhs=qT[64:128, :, c0:c0 + 128],
                                           start=True, stop=True), ("kT", slot))
                    P.pe(fn, reads=[("kT", slot), ("qT", c0)], writes=[("ps", b0), ("ps", b1)])
                    mcol = hmask if (use_mask and c == 0) else zcol
                    for hf, b in enumerate((b0, b1)):
                        P.dve(lambda e, b=b, hf=hf, c=c, mcol=mcol: e.scalar_tensor_tensor(
                            out=sc[:, hf * 512:(hf + 1) * 512], in0=ps[b][:, :], scalar=mcol[:, 0:1],
                            in1=biasT[:, c, hf * 4:(hf + 1) * 4, :].rearrange("p h q -> p (h q)"), op0=ALU.add, op1=ALU.add),
                            reads=[("ps", b), "hmask", "zcol"] + BIAS, writes=[("sc", hf)])
                    P.act(lambda e, c=c: e.activation(out=PT[:, c, :], in_=sc[:, :], func=AF.Exp),
                          reads=[("sc", 0), ("sc", 1)], writes=[("PT", c)])
                bo = bank(); bd = bank()

                def fn(e, bo=bo, bd=bd, pslot=pslot, oslot=oslot):
                    ins = None
                    for g in range(2):
                        for c, slot in enumerate((pslot, oslot)):
                            ins = ST(e.matmul(ps[bo][g * 64:(g + 1) * 64, :], lhsT=Vb[:, slot, g * 64:(g + 1) * 64],
                                              rhs=PT[:, c, g * 512:(g + 1) * 512], start=(c == 0), stop=(c == 1),
                                              tile_position=(0, g * 64)), ("V", slot))
                    for g in range(2):
                        for c in range(2):
                            ins = ST(e.matmul(ps[bd][g * 64:(g + 1) * 64, :], lhsT=onesb[:, :],
                                              rhs=PT[:, c, g * 512:(g + 1) * 512], start=(c == 0), stop=(c == 1),
                                              tile_position=(0, g * 64)), "onesb")
                    return ins
                P.pe(fn, reads=[("V", pslot), ("V", oslot), ("PT", 0), ("PT", 1), "onesb"], writes=[("ps", bo), ("ps", bd)])
                P.dve(lambda e, bd=bd: e.tensor_tensor(
                    out=dtmp[:, :].rearrange("p (j q) -> p j q", j=4), in0=ps[bd][:, :].rearrange("p (j q) -> p j q", j=4),
                    in1=esink[:, :].unsqueeze(2).to_broadcast([128, 4, 128]), op=ALU.add),
                    reads=[("ps", bd), "esink", "esink2"], writes=["dtmp"])
                P.dve(lambda e: e.reciprocal(out=dtmp[:, :], in_=dtmp[:, :]), reads=["dtmp"], writes=["dtmp"])
                P.dve(lambda e, bo=bo, c0=c0: e.tensor_tensor(
                    out=aT[:, :, c0:c0 + 128], in0=ps[bo][:, :].rearrange("p (j q) -> p j q", j=4),
                    in1=dtmp[:, :].rearrange("p (j q) -> p j q", j=4), op=ALU.mult),
                    reads=[("ps", bo), "dtmp"], writes=[("aT", c0)])
            lm = [s for s in T["subs"] if s["kind"] == "main"][-1]
            ls = 1 + lm["c0"] // 128
            if T["t"] < 3:
                P.act(lambda e, ls=ls: e.activation(out=kT[:, 0:128], in_=kT[:, ls * 128:(ls + 1) * 128], func=AF.Copy),
                      reads=[("kT", ls)], writes=[("kT", 0)])
                P.act(lambda e, ls=ls: e.activation(out=Vb[:, 0, :], in_=Vb[:, ls, :], func=AF.Copy),
                      reads=[("V", ls)], writes=[("V", 0)])

        def sample_attn():
            SC0 = 512
            P.dma("sync", lambda e: e.dma_start(out=nks[:, 0:127, :], in_=ck[:, 1:128, :]), writes=["o_nks_old"])
            P.dma("sync", lambda e: e.dma_start(out=nvs[:, 0:127, :], in_=cv[:, 1:128, :]), writes=["o_nvs_old"])
            P.dma("pool", lambda e: e.dma_start(out=Ks[:, :, :], in_=nks.rearrange("i j d -> j i d")),
                  reads=["o_nks_old", "o_nks_new"], writes=["Ks"])
            P.dma("pool", lambda e: e.dma_start(out=Vs[:, :, :], in_=nvs.rearrange("i j d -> j i d")),
                  reads=["o_nvs_old", "o_nvs_new"], writes=["Vs"])
            if SUBCUT < 2:
                return
            for g4 in range(4):
                b = bank()
                pb = ps[b][:, :].bitcast(BF16)

                def fn(e, g4=g4, pb=pb):
                    ins = None
                    for ii in range(4):
                        ins = ST(e.transpose(pb[:, ii * 128:(ii + 1) * 128], Ks[:, g4 * 4 + ii, :], identb[:, :]), "Ks")
                    return ins
                P.pe(fn, reads=["Ks", "identb"], writes=[("ps", b)])
                P.act(lambda e, g4=g4, pb=pb: e.activation(out=KTs[:, g4 * 4:(g4 + 1) * 4, :],
                                                           in_=pb[:, 0:512].rearrange("p (i n) -> p i n", i=4), func=AF.Copy),
                      reads=[("ps", b)], writes=[("KTs", g4)])
            if SUBCUT < 3:
                return
            bA = bank(); bB = bank()

            def fn(e, bA=bA, bB=bB):
                ins = None
                for i in range(16):
                    e.matmul(ps[bA][:, i * 4:i * 4 + 4], lhsT=KTs[0:64, i, :], rhs=qT[0:64, :, SC0 + i], start=True, stop=True)
                    ins = e.matmul(ps[bB][:, i * 4:i * 4 + 4], lhsT=KTs[64:128, i, :], rhs=qT[64:128, :, SC0 + i], start=True, stop=True)
                return ins
            P.pe(fn, reads=[("KTs", g4) for g4 in range(4)] + [("qT", SC0)], writes=[("ps", bA), ("ps", bB)])
            for l, bX in enumerate((bA, bB)):
                P.dve(lambda e, l=l, bX=bX: e.tensor_tensor(
                    out=sc[:, 0:128].rearrange("p (i h) -> p i h", h=8)[:, :, l * 4:(l + 1) * 4],
                    in0=ps[bX][:, 0:64].rearrange("p (i j) -> p i j", j=4),
                    in1=biasT[:, 1, l * 4:(l + 1) * 4, 127].unsqueeze(1).to_broadcast([128, 16, 4]), op=ALU.add),
                    reads=[("ps", bX)] + BIAS, writes=[("sc", l)])
            P.act(lambda e: e.activation(out=PT[:, 0, 0:128], in_=sc[:, 0:128], func=AF.Exp), reads=[("sc", 0), ("sc", 1)], writes=[("PT", 0)])
            if SUBCUT < 4:
                return
            bo = bank(); bd = bank()
            ptv = PT[:, 0, 0:128].rearrange("p (i l j) -> p i l j", l=2, j=4)

            def fn(e, bo=bo, bd=bd):
                ins = None
                for i in range(16):
                    for g in range(2):
                        ins = ST(e.matmul(ps[bo][g * 64:(g + 1) * 64, i * 4:(i + 1) * 4], lhsT=Vs[:, i, g * 64:(g + 1) * 64],
                                          rhs=ptv[:, i, g, :], start=True, stop=True, tile_position=(0, g * 64)), "Vs")
                for g in range(2):
                    ins = ST(e.matmul(ps[bd][g * 64:(g + 1) * 64, 0:64].rearrange("p (i j) -> p i j", j=4), lhsT=onesb[:, :],
                                      rhs=ptv[:, :, g, :], start=True, stop=True, tile_position=(0, g * 64)), "onesb")
                return ins
            P.pe(fn, reads=["Vs", ("PT", 0), "onesb"], writes=[("ps", bo), ("ps", bd)])
            if SUBCUT < 5:
                return
            P.dve(lambda e, bd=bd: e.tensor_tensor(
                out=dtmp[:, 0:64].rearrange("p (i j) -> p i j", j=4), in0=ps[bd][:, 0:64].rearrange("p (i j) -> p i j", j=4),
                in1=esink[:, :].unsqueeze(1).to_broadcast([128, 16, 4]), op=ALU.add),
                reads=[("ps", bd), "esink", "esink2"], writes=["dtmp"])
            P.dve(lambda e: e.reciprocal(out=dtmp[:, 0:64], in_=dtmp[:, 0:64]), reads=["dtmp"], writes=["dtmp"])
            P.dve(lambda e, bo=bo: e.tensor_tensor(
                out=aT[:, :, SC0:SC0 + 16], in0=ps[bo][:, 0:64].rearrange("p (i j) -> p j i", j=4),
                in1=dtmp[:, 0:64].rearrange("p (i j) -> p j i", j=4), op=ALU.mult),
                reads=[("ps", bo), "dtmp"], writes=[("aT", SC0)])

        def conv_stage(T):
            c0, n = T["main_cg"]
            last = T["t"] == 3
            gk = []
            for c in range(4):
                gk.append([("glu", c), ("gluc", c)])
            if last:
                bs = bank()
                for r in range(4):
                    P.dma("sync", lambda e, r=r: e.dma_start(out=st[:, :], in_=stc[r * 4:(r + 1) * 4, :, :].rearrange("a j c -> (a j) c")),
                          writes=["st"])
                    P.dve(lambda e: e.tensor_tensor(out=st[:, :], in0=st[:, :], in1=wrep[:, :], op=ALU.mult),
                          reads=["st"] + [("wrep", a) for a in range(4)], writes=["st"])

                    def fn(e, r=r, bs=bs):
                        ins = None
                        for c in range(4):
                            ins = ST(e.matmul(ps[bs][:, c * 16 + r * 4:c * 16 + r * 4 + 4], lhsT=st[:, c * 128:(c + 1) * 128], rhs=sel[:, :],
                                              start=True, stop=True), "st")
                        return ins
                    P.pe(fn, reads=["st", "sel"], writes=[("ps", bs)])
                P.dma("sync", lambda e: e.dma_start(out=ncs[:, 0:29, :], in_=stc[:, 1:30, :]), writes=["o_ncs_old"])
            for c in range(4):
                diag = diags[c % 2]
                P.dve(lambda e, c=c, diag=diag: e.tensor_tensor(
                    out=diag[:, :, :], in0=identf[:, :].unsqueeze(1).to_broadcast([128, 31, 128]),
                    in1=wdwT[:, c, :].unsqueeze(2).to_broadcast([128, 31, 128]), op=ALU.mult),
                    reads=["identf", "wdwT"], writes=["diag%d" % (c % 2)])
                b = bank()

                def fn(e, b=b, c=c, diag=diag):
                    ins = None
                    for j in range(31):
                        ins = e.matmul(ps[b][:, :], lhsT=diag[:, j, :], rhs=gluT[:, c, c0 + j:c0 + j + 512], start=(j == 0), stop=(j == 30))
                    return ins
                P.pe(fn, reads=["diag%d" % (c % 2)] + gk[c], writes=[("ps", b)])
                P.act(lambda e, b=b, c=c: e.activation(out=yb[:, c, 0:512], in_=ps[b][:, :], func=AF.Identity, bias=cvec[:, 0, c:c + 1]),
                      reads=[("ps", b), "cv0"], writes=[("y", c, 0)])
                if last:
                    P.dve(lambda e, c=c: e.scalar_tensor_tensor(
                        out=yb[:, c, 512:528], in0=glu32[:, c, 30:46], scalar=wdwT[:, c, 30:31], in1=ps[bs][:, c * 16:(c + 1) * 16],
                        op0=ALU.mult, op1=ALU.add), reads=[("glu32s", c), "wdwT", ("ps", bs)], writes=[("y", c, 1)])
                    P.dve(lambda e, c=c: e.tensor_scalar(out=yb[:, c, 512:528], in0=yb[:, c, 512:528], scalar1=cvec[:, 0, c:c + 1],
                                                         scalar2=None, op0=ALU.add), reads=[("y", c, 1), "cv0"], writes=[("y", c, 1)])
            ntot = 528 if last else 512
            YK = [("y", c, 0) for c in range(4)] + ([("y", c, 1) for c in range(4)] if last else [])
            b1 = bank(); b2 = bank()
            for c in range(4):
                yq = ysq[c % 2]
                P.act(lambda e, c=c, yq=yq: e.activation(out=yq[:, 0:ntot] if ntot <= 512 else yq[:, 0:512], in_=yb[:, c, 0:min(ntot, 512)], func=AF.Square),
                      reads=YK, writes=["ysq%d" % (c % 2)])

                def fn(e, c=c, yq=yq):
                    ST(e.matmul(ps[b1][:, :], lhsT=onesf[:, :], rhs=yb[:, c, 0:512], start=(c == 0), stop=(c == 3)), "onesf")
                    return ST(e.matmul(ps[b2][:, :], lhsT=onesf[:, :], rhs=yq[:, 0:512], start=(c == 0), stop=(c == 3)), "onesf")
                P.pe(fn, reads=YK + ["ysq%d" % (c % 2), "onesf"], writes=[("ps", b1), ("ps", b2)])
            cgl = [(0, 512, b1, b2)]
            if last:
                b3 = bank(); b4 = bank()
                for c in range(4):
                    yq = ysq[c % 2]
                    P.act(lambda e, c=c, yq=yq: e.activation(out=yq[:, 0:16], in_=yb[:, c, 512:528], func=AF.Square),
                          reads=YK, writes=["ysq%d" % (c % 2)])

                    def fn(e, c=c, yq=yq):
                        ST(e.matmul(ps[b3][:, 0:16], lhsT=onesf[:, :], rhs=yb[:, c, 512:528], start=(c == 0), stop=(c == 3)), "onesf")
                        return ST(e.matmul(ps[b4][:, 0:16], lhsT=onesf[:, :], rhs=yq[:, 0:16], start=(c == 0), stop=(c == 3)), "onesf")
                    P.pe(fn, reads=YK + ["ysq%d" % (c % 2), "onesf"], writes=[("ps", b3), ("ps", b4)])
                cgl.append((512, 16, b3, b4))
            for (o, nn, ba, bb) in cgl:
                P.dve(lambda e, o=o, nn=nn, ba=ba: e.tensor_scalar(out=mean[:, o:o + nn], in0=ps[ba][:, 0:nn], scalar1=1.0 / 512, scalar2=None, op0=ALU.mult),
                      reads=[("ps", ba)], writes=[("mean", o)])
                P.dve(lambda e, o=o, nn=nn: e.tensor_tensor(out=var[:, o:o + nn], in0=mean[:, o:o + nn], in1=mean[:, o:o + nn], op=ALU.mult),
                      reads=[("mean", o)], writes=[("var", o)])
                P.dve(lambda e, o=o, nn=nn, bb=bb: e.scalar_tensor_tensor(out=var[:, o:o + nn], in0=ps[bb][:, 0:nn], scalar=1.0 / 512, in1=var[:, o:o + nn],
                                                                          op0=ALU.mult, op1=ALU.subtract),
                      reads=[("ps", bb), ("var", o)], writes=[("var", o)])
                P.act(lambda e, o=o, nn=nn: e.activation(out=var[:, o:o + nn], in_=var[:, o:o + nn], func=AF.Sqrt, bias=epsc[:, 0:1]),
                      reads=[("var", o), "epsc"], writes=[("var", o)])
                P.dve(lambda e, o=o, nn=nn: e.reciprocal(out=var[:, o:o + nn], in_=var[:, o:o + nn]), reads=[("var", o)], writes=[("var", o)])
                for c in range(4):
                    yk = ("y", c, 0 if o == 0 else 1)
                    P.dve(lambda e, o=o, nn=nn, c=c: e.tensor_tensor(out=yb[:, c, o:o + nn], in0=yb[:, c, o:o + nn], in1=mean[:, o:o + nn], op=ALU.subtract),
                          reads=[yk, ("mean", o)], writes=[yk])
                    P.dve(lambda e, o=o, nn=nn, c=c: e.tensor_tensor(out=yb[:, c, o:o + nn], in0=yb[:, c, o:o + nn], in1=var[:, o:o + nn], op=ALU.mult),
                          reads=[yk, ("var", o)], writes=[yk])
                    P.act(lambda e, o=o, nn=nn, c=c: e.activation(out=sw[:, c, o:o + nn], in_=yb[:, c, o:o + nn], func=AF.Silu,
                                                                  scale=cvec[:, 1, c:c + 1], bias=cvec[:, 2, c:c + 1]),
                          reads=[yk, "cv1", "cv2"], writes=[("sw", c, o)])
            if not last:
                for c in range(4):
                    P.act(lambda e, c=c: e.activation(out=gluT[:, c, 0:30], in_=gluT[:, c, 30 + c0 + 482:30 + c0 + 512], func=AF.Copy),
                          reads=[("glu", c)], writes=[("gluc", c)])
            else:
                b = bank()

                def fn(e, b=b):
                    ins = None
                    for c in range(4):
                        ins = ST(e.transpose(ps[b][0:30, c * 128:(c + 1) * 128], glu32[:, c, 0:30], identf[:, :]), ("glu32m", c))
                    return ins
                P.pe(fn, reads=[("glu32m", c) for c in range(4)] + ["identf"], writes=[("ps", b)])
                P.act(lambda e, b=b: e.activation(out=otok[0:30, :], in_=ps[b][0:30, :], func=AF.Copy), reads=[("ps", b)], writes=["otok"])
                P.dma("sync", lambda e: e.dma_start(out=ncp, in_=otok[0:30, :]), reads=["otok"], writes=["o_ncp"])
                b = bank()

                def fn(e, b=b):
                    ins = None
                    for c in range(4):
                        ins = ST(e.transpose(ps[b][0:16, c * 128:(c + 1) * 128], glu32[:, c, 30:46], identf[:, :]), ("glu32s", c))
                    return ins
                P.pe(fn, reads=[("glu32s", c) for c in range(4)] + ["identf"], writes=[("ps", b)])
                P.act(lambda e, b=b: e.activation(out=otok[0:16, :], in_=ps[b][0:16, :], func=AF.Copy), reads=[("ps", b), "o_ncp"], writes=["otok"])
                P.dma("sync", lambda e: e.dma_start(out=ncs[:, 29, :], in_=otok[0:16, :]), reads=["otok"], writes=["o_ncs_new"])

        def post_stage(T):
            cgs = T["cgs_ms"]
            for i in range(8):
                k, s = w_use()
                wga = ring[:, s, 0:1024].rearrange("p (k n) -> p k n", k=8)
                wgc = ring[:, s, 1024:2048].rearrange("p (k n) -> p k n", k=8)
                wao = ring[:, s, 2048:2560].rearrange("p (j n) -> p j n", j=4)
                wco = ring[:, s, 2560:3072].rearrange("p (k n) -> p k n", k=4)
                for cg in cgs:
                    c0, n = cg
                    o = 0 if n == 512 else 512
                    so = c0 if n == 512 else 512
                    bA = bank(); bC = bank(); ba = bank(); bc = bank()

                    def fn(e, bA=bA, bC=bC, ba=ba, bc=bc, c0=c0, n=n, o=o, wga=wga, wgc=wgc, wao=wao, wco=wco, s=s):
                        ins = None
                        for kk in range(8):
                            ins = ST(e.matmul(ps[bA][:, 0:n], lhsT=wga[:, kk, :], rhs=UT[:, kk, c0:c0 + n], start=(kk == 0), stop=(kk == 7)), ("ring", s, 0))
                        for kk in range(8):
                            ins = ST(e.matmul(ps[bC][:, 0:n], lhsT=wgc[:, kk, :], rhs=UT[:, kk, c0:c0 + n], start=(kk == 0), stop=(kk == 7)), ("ring", s, 1))
                        for j in range(4):
                            ins = e.matmul(ps[ba][:, 0:n], lhsT=wao[:, j, :], rhs=aT[:, j, c0:c0 + n], start=(j == 0), stop=(j == 3))
                        for c in range(4):
                            ins = ST(e.matmul(ps[bc][:, 0:n], lhsT=wco[:, c, :], rhs=sw[:, c, o:o + n], start=(c == 0), stop=(c == 3)), ("ring", s, 4))
                        return ins
                    akeys = [("aT", s_["c0"]) for s_ in T["subs"] if s_["kind"] != "halo" and c0 <= s_["c0"] < c0 + n]
                    P.pe(fn, reads=RK(s) + ut_keys(T, cg) + akeys + [("sw", c, o) for c in range(4)],
                         writes=[("ps", bA), ("ps", bC), ("ps", ba), ("ps", bc)])
                    P.act(lambda e, bA=bA, n=n: e.activation(out=gA[:, 0:n], in_=ps[bA][:, 0:n], func=AF.Sigmoid), reads=[("ps", bA)], writes=["gA"])
                    P.act(lambda e, bC=bC, n=n: e.activation(out=gC[:, 0:n], in_=ps[bC][:, 0:n], func=AF.Sigmoid), reads=[("ps", bC)], writes=["gC"])
                    P.dve(lambda e, ba=ba, n=n: e.tensor_tensor(out=gA[:, 0:n], in0=gA[:, 0:n], in1=ps[ba][:, 0:n], op=ALU.mult),
                          reads=["gA", ("ps", ba)], writes=["gA"])
                    P.dve(lambda e, bc=bc, n=n: e.tensor_tensor(out=gC[:, 0:n], in0=gC[:, 0:n], in1=ps[bc][:, 0:n], op=ALU.mult),
                          reads=["gC", ("ps", bc)], writes=["gC"])
                    P.dve(lambda e, n=n, o=o, i=i: e.tensor_tensor(out=mT[:, i, o:o + n], in0=gA[:, 0:n], in1=gC[:, 0:n], op=ALU.add),
                          reads=["gA", "gC"], writes=[("mT", i, o)])
                w_release(k)
            ws = [w_use() for _ in range(3)]
            spans = ((0, 384), (384, 768), (768, 1024))
            for sub in T["subs"]:
                if sub["kind"] == "halo":
                    continue
                nt = sub["n"]
                o = 512 if sub["kind"] == "samp" else sub["c0"] - T["off"]
                og = 512 if sub["kind"] == "samp" else 0
                for (k, s), (a, bnd) in zip(ws, spans):
                    wdt = bnd - a
                    b = bank()
                    wv = ring[:, s, 0:8 * wdt].rearrange("p (k n) -> p k n", k=8)

                    def fn(e, b=b, nt=nt, o=o, wv=wv, wdt=wdt, og=og):
                        ins = None
                        for kk in range(8):
                            ins = ST(e.matmul(ps[b][0:nt, 0:wdt], lhsT=mT[:, kk, o:o + nt], rhs=wv[:, kk, :], start=(kk == 0), stop=(kk == 7)), ("mT", kk, og))
                        return ins
                    P.pe(fn, reads=RK(s) + [("mT", i, og) for i in range(8)], writes=[("ps", b)])
                    P.dve(lambda e, b=b, nt=nt, sub=sub, a=a, bnd=bnd, wdt=wdt: e.tensor_tensor(
                        out=H[0:nt, sub["h"], a:bnd], in0=ps[b][0:nt, 0:wdt], in1=H[0:nt, sub["h"], a:bnd], op=ALU.add),
                        reads=[("ps", b), ("H", sub["h"])], writes=[("H", sub["h"])])
            for (k, s) in ws:
                w_release(k)

        for T in tiles[:MAXT]:
            t = T["t"]
            lastT = (t == MAXT - 1)
            def on(k):
                return (not lastT) or STAGES >= k
            for sub in T["subs"]:
                if sub["kind"] == "halo":
                    src = xh
                elif sub["kind"] == "samp":
                    src = xs
                else:
                    src = xm[sub["blk"] * 128:(sub["blk"] + 1) * 128, :]
                P.dma("sync", lambda e, sub=sub, src=src: e.dma_start(out=H[0:sub["n"], sub["h"], :], in_=src), writes=[("H", sub["h"])])
            def dbg_dump(dst):
                for sub in T["subs"]:
                    if sub["kind"] == "main":
                        P.dma("sync", lambda e, sub=sub: e.dma_start(out=dst[sub["blk"] * 128:(sub["blk"] + 1) * 128, :], in_=H[:, sub["h"], :]),
                              reads=[("H", sub["h"])], writes=[("dbg", id(dst), sub["blk"])])
            if DEBUG_OUT:
                dbg_dump(dbg3)
            norm_stage(T, 0, T["subs"])
            if on(1):
                ffn_stage(T, T["cgs_all"], T["subs"])
            if DEBUG_OUT:
                dbg_dump(dbg1)
            if not on(2):
                continue
            norm_stage(T, 1, T["subs"])
            def dump(name, tens, shape, dt, rkeys):
                if not DEBUG_OUT:
                    return
                if name not in DEBUG_NAMES:
                    return
                d = nc.dram_tensor("dbg_%s_%d" % (name, t), list(shape), F32, kind="ExternalOutput").ap()
                P.dma("pool", lambda e: e.dma_start(out=d, in_=tens), reads=rkeys, writes=[("dbgx", name, t)])
            qkv_stage(T)
            if not on(3):
                continue
            dump("qT", qT[:, :, :], [128, 4, 640], BF16, [("qT", s_["c0"]) for s_ in T["subs"]])
            dump("kT", kT[:, :], [128, 768], BF16, [("kT", i) for i in range(6)])
            dump("Vb", Vb[:, :, :], [128, 6, 128], BF16, [("V", i) for i in range(6)])
            glu_stage(T)
            if not on(4):
                continue
            dump("glu", gluT[:, :, :], [128, 4, 670], BF16, [("glu", c) for c in range(4)] + [("gluc", c) for c in range(4)])
            attn_stage(T)
            if not on(5):
                continue
            if t == 3:
                sample_attn()
            dump("aT", aT[:, :, :], [128, 4, 640], BF16, [("aT", s_["c0"]) for s_ in T["subs"]])
            if not on(6):
                continue
            conv_stage(T)
            if not on(7):
                continue
            dump("sw", sw[:, :, :], [128, 4, 528], BF16, [("sw", c, o) for c in range(4) for o in (0, 512)])
            post_stage(T)
            if not on(8):
                continue
            dump("mT", mT[:, :, :], [128, 8, 528], BF16, [("mT", i, o) for i in range(8) for o in (0, 512)])
            if DEBUG_OUT:
                dbg_dump(dbg2)
            subs2 = [s for s in T["subs"] if s["kind"] != "halo"]
            norm_stage(T, 2, subs2)
            ffn_stage(T, T["cgs_ms"], subs2)
            for sub in subs2:
                if sub["kind"] == "samp":
                    dst = y_s
                else:
                    dst = y_m[sub["blk"] * 128:(sub["blk"] + 1) * 128, :]
                P.dma("sync", lambda e, sub=sub, dst=dst: e.dma_start(out=dst, in_=H[0:sub["n"], sub["h"], :]),
                      reads=[("H", sub["h"])], writes=[("o_y", sub["h"])])
        assert MAXT < 4 or STAGES < 99 or cur[0] == n_pieces, (cur[0], n_pieces)
        _NC_CACHE['P'] = P
        P.emit(es)
    return nc


_NC_CACHE = {}


def _t5_bucket_np(d):
    import math
    max_exact = 16
    d = np.maximum(d, 0)
    ratio = np.log(np.maximum(d, 1).astype(np.float32) / max_exact) / np.float32(math.log(128 / max_exact))
    large = np.minimum(max_exact + (ratio * (32 - max_exact)).astype(np.int32), 31)
    return np.where(d < max_exact, d, large)


def _consts():
    identb = np.eye(128, dtype=np.float32).astype(ml_dtypes.bfloat16)
    identf = np.eye(128, dtype=np.float32)
    er = np.zeros((33, 383), np.float32)
    for m in range(383):
        d = m - 127
        if 0 <= d < 128:
            er[int(_t5_bucket_np(np.array([d]))[0]), m] = 1.0
        else:
            er[32, m] = 1.0
    sel = np.zeros((120, 4), np.float32)
    for p in range(120):
        sel[p, p // 30] = 1.0
    return identb, identf, er, sel


def make_in_maps(inputs):
    f = lambda k: np.ascontiguousarray(np.asarray(inputs[k], dtype=np.float32))
    x_prompt = f("x_prompt"); x_sample = f("x_sample"); cache_k = f("cache_k"); cache_v = f("cache_v")
    state_conv = f("state_conv"); meta = f("meta_tokens")
    identb, identf, er, sel = _consts()
    wnames = ["ffn1_norm", "ffn1_w1", "ffn1_w3", "ffn1_w2", "mix_norm", "w_in", "q_norm", "k_norm", "rel_bias", "sinks",
              "w_attn_out", "w_dw", "b_dw", "conv_ln_g", "conv_ln_b", "w_conv_out", "w_out", "ffn2_norm", "ffn2_w1",
              "ffn2_w3", "ffn2_w2"]
    wts = {k: f(k) for k in wnames}
    in_maps = []
    for c in range(8):
        b, cc = divmod(c, 4)
        m = dict(wts)
        m["xm"] = np.ascontiguousarray(x_prompt[b, cc * 2048:(cc + 1) * 2048])
        if cc == 0:
            xh = np.zeros((128, D), np.float32)
            xh[112:] = meta
            hm = np.zeros((128, 1), np.float32)
            hm[:112] = NEG
        else:
            xh = np.ascontiguousarray(x_prompt[b, cc * 2048 - 128:cc * 2048])
            hm = np.zeros((128, 1), np.float32)
        m["xh"] = xh
        m["hmask"] = hm
        m["xs"] = np.ascontiguousarray(x_sample[c * 16:(c + 1) * 16, 0, :])
        m["ck"] = np.ascontiguousarray(cache_k[c * 16:(c + 1) * 16].reshape(16, 128, 128))
        m["cv"] = np.ascontiguousarray(cache_v[c * 16:(c + 1) * 16].reshape(16, 128, 128))
        m["stc"] = np.ascontiguousarray(state_conv[c * 16:(c + 1) * 16])
        m["identb"] = identb; m["identf"] = identf; m["er"] = er; m["sel"] = sel
        in_maps.append(m)
    return in_maps


def kernel(**inputs):
    if "nc" not in _NC_CACHE:
        _NC_CACHE["nc"] = build_nc()
    nc = _NC_CACHE["nc"]
    in_maps = make_in_maps(inputs)
    res = run_bass_kernel_spmd(nc, in_maps, core_ids=list(range(8)))
    R = res.results
    y_prompt = np.stack([np.concatenate([R[b * 4 + cc]["y_m"] for cc in range(4)], axis=0) for b in range(2)], axis=0)
    y_sample = np.concatenate([R[c]["y_s"] for c in range(8)], axis=0).reshape(128, 1, D)
    nkp = np.stack([R[3]["nkp"], R[7]["nkp"]], axis=0).reshape(2, 128, 2, 64)
    nvp = np.stack([R[3]["nvp"], R[7]["nvp"]], axis=0).reshape(2, 128, 2, 64)
    ncp = np.stack([R[3]["ncp"], R[7]["ncp"]], axis=0)
    nks = np.concatenate([R[c]["nks"] for c in range(8)], axis=0).reshape(128, 128, 2, 64)
    nvs = np.concatenate([R[c]["nvs"] for c in range(8)], axis=0).reshape(128, 128, 2, 64)
    ncs = np.concatenate([R[c]["ncs"] for c in range(8)], axis=0)
    return tuple(np.asarray(a, dtype=np.float32) for a in (y_prompt, y_sample, nkp, nvp, ncp, nks, nvs, ncs))
```

```python
import contextlib
import numpy as np
import ml_dtypes
import concourse.bass as bass
import concourse.mybir as mybir
from concourse.bass_utils import run_bass_kernel_spmd

F32 = mybir.dt.float32
BF16 = mybir.dt.bfloat16
ALU = mybir.AluOpType
AF = mybir.ActivationFunctionType
AX = mybir.AxisListType

COMPUTE = ("pe", "act", "dve", "pool")
NEG = -1e30
EPS = 1e-6
D = 1024
DFF = 2816
NCH = 22
NS = 6
GS = 3


class Op:
    __slots__ = ("idx", "stream", "fn", "reads", "writes", "is_dma", "deps", "flag", "ts", "sem")

    def __init__(self, idx, stream, fn, reads, writes, is_dma):
        self.idx = idx
        self.stream = stream
        self.fn = fn
        self.reads = reads
        self.writes = writes
        self.is_dma = is_dma
        self.deps = set()
        self.flag = False
        self.ts = None
        self.sem = None


_CUR = {}
ATTACH_WAITS = False
DEBUG_OUT = False
MAXT = 4
STAGES = 99
SUBCUT = 99
DEBUG_NAMES = ()


def ST(ins, *keys):
    if not ATTACH_WAITS:
        return ins
    op = _CUR["op"]; ops = _CUR["ops"]
    best = {}
    for k in keys:
        cand = [ops[d] for d in op.deps if k in ops[d].writes]
        if not cand:
            continue
        p = max(cand, key=lambda o: o.idx)
        if p.sem is None or p.stream == "pe" and not p.is_dma:
            continue
        sid = id(p.sem)
        if sid not in best or best[sid][1] < p.ts:
            best[sid] = (p.sem, p.ts)
    assert len(best) <= 1, (keys, best)
    for sem, ts in best.values():
        ins._wait_ge(sem, ts)
    return ins


def _same_eng_hazard(p, op):
    pw = set(p.writes)
    return bool(pw & set(op.reads)) or bool(pw & set(op.writes)) or bool(set(p.reads) & set(op.writes))


class Prog:
    def __init__(self, nc, n_dma_sems=20):
        self.nc = nc
        self.ops = []
        self.last_w = {}
        self.readers = {}
        self.n_dma_sems = n_dma_sems

    def _add(self, stream, fn, reads, writes, is_dma):
        op = Op(len(self.ops), stream, fn, tuple(reads), tuple(writes), is_dma)
        for k in op.reads:
            w = self.last_w.get(k)
            if w is not None:
                op.deps.add(w)
        for k in op.writes:
            w = self.last_w.get(k)
            if w is not None:
                op.deps.add(w)
            for r in self.readers.get(k, ()):
                op.deps.add(r)
        for k in op.reads:
            self.readers.setdefault(k, []).append(op.idx)
        for k in op.writes:
            self.last_w[k] = op.idx
            self.readers[k] = []
        op.deps.discard(op.idx)
        self.ops.append(op)
        return op

    def pe(self, fn, reads=(), writes=()):
        return self._add("pe", fn, reads, writes, False)

    def act(self, fn, reads=(), writes=()):
        return self._add("act", fn, reads, writes, False)

    def dve(self, fn, reads=(), writes=()):
        return self._add("dve", fn, reads, writes, False)

    def pool(self, fn, reads=(), writes=()):
        return self._add("pool", fn, reads, writes, False)

    def dma(self, queue, fn, reads=(), writes=()):
        return self._add(queue, fn, reads, writes, True)

    def emit(self, es):
        nc = self.nc
        ops = self.ops
        streams = ["pe", "act", "dve", "pool", "sync"]
        for op in ops:
            for d in op.deps:
                p = ops[d]
                if p.is_dma:
                    continue
                if p.stream != op.stream or op.is_dma:
                    p.flag = True
                elif p.stream in ("act", "dve", "pool"):
                    if _same_eng_hazard(p, op):
                        p.flag = True
        csem = {s: es.enter_context(nc.semaphore("c_" + s)) for s in COMPUTE}
        dsems = {}
        nds = {"sync": 12, "pool": 14}
        for q in ("sync", "pool"):
            dsems[q] = [es.enter_context(nc.semaphore("d_%s_%d" % (q, i))) for i in range(nds[q])]
        ccount = {s: 0 for s in COMPUTE}
        dcount = {q: [0] * nds[q] for q in dsems}
        drr = {q: 0 for q in dsems}
        dprev = {}
        for op in ops:
            if op.is_dma:
                q = op.stream
                i = drr[q]
                drr[q] = (i + 1) % nds[q]
                prev = dcount[q][i]
                dcount[q][i] = prev + 16
                op.sem = dsems[q][i]
                op.ts = prev + 16
                op.flag = True
                dprev[op.idx] = prev
            elif op.flag:
                ccount[op.stream] += 1
                op.sem = csem[op.stream]
                op.ts = ccount[op.stream]
        by_stream = {s: [o for o in ops if o.stream == s] for s in streams}
        block = es.enter_context(nc.Block())

        def make(stream):
            def body(eng):
                seen = {}
                for op in by_stream[stream]:
                    waits = {}
                    for d in op.deps:
                        p = ops[d]
                        if not p.flag:
                            continue
                        if (not p.is_dma) and p.stream == stream and not op.is_dma:
                            if stream == "pe":
                                continue
                            if not _same_eng_hazard(p, op):
                                continue
                        key = id(p.sem)
                        if waits.get(key, (None, 0))[1] < p.ts:
                            waits[key] = (p.sem, p.ts)
                    if op.is_dma and dprev[op.idx] > 0:
                        key = id(op.sem)
                        if waits.get(key, (None, 0))[1] < dprev[op.idx]:
                            waits[key] = (op.sem, dprev[op.idx])
                    for key, (sem, ts) in waits.items():
                        if seen.get(key, 0) >= ts:
                            continue
                        eng.wait_ge(sem, ts)
                        seen[key] = ts
                    _CUR["op"] = op; _CUR["ops"] = ops
                    ins = op.fn(eng)
                    if op.is_dma:
                        ins.then_inc(op.sem, 16)
                    elif op.flag:
                        ins.then_inc(op.sem, 1)
                if stream == "sync":
                    for q in dsems:
                        for i, sem in enumerate(dsems[q]):
                            if dcount[q][i] > 0:
                                eng.wait_ge(sem, dcount[q][i])
            return body

        block.tensor(make("pe"))
        block.scalar(make("act"))
        block.vector(make("dve"))
        block.gpsimd(make("pool"))
        block.sync(make("sync"))


def _tiles():
    tiles = []
    for t in range(4):
        subs = []
        off = 128 if t == 0 else 0
        if t == 0:
            subs.append(dict(kind="halo", c0=0, n=128, h=4, blk=-1))
        for i in range(4):
            subs.append(dict(kind="main", c0=off + 128 * i, n=128, h=i, blk=4 * t + i))
        if t == 3:
            subs.append(dict(kind="samp", c0=512, n=16, h=4, blk=-1))
        main_cg = (off, 512)
        cgs_all = [main_cg]
        cgs_ms = [main_cg]
        if t == 0:
            cgs_all.append((0, 128))
        if t == 3:
            cgs_all.append((512, 16))
            cgs_ms.append((512, 16))
        tiles.append(dict(t=t, subs=subs, main_cg=main_cg, cgs_all=cgs_all, cgs_ms=cgs_ms, off=off))
    return tiles


def build_nc():
    nc = bass.Bass("TRN2", target_bir_lowering=False)

    def din(name, shape, dt=F32):
        return nc.dram_tensor(name, list(shape), dt, kind="ExternalInput").ap()

    def dout(name, shape):
        return nc.dram_tensor(name, list(shape), F32, kind="ExternalOutput").ap()

    xh = din("xh", [128, D]); xm = din("xm", [2048, D]); xs = din("xs", [16, D])
    ck = din("ck", [16, 128, 128]); cv = din("cv", [16, 128, 128]); stc = din("stc", [16, 30, 512])
    hmask_d = din("hmask", [128, 1])
    g1 = din("ffn1_norm", [D]); w11 = din("ffn1_w1", [D, DFF]); w13 = din("ffn1_w3", [D, DFF]); w12 = din("ffn1_w2", [DFF, D])
    gm = din("mix_norm", [D]); w_in = din("w_in", [D, 3840]); qn_d = din("q_norm", [64]); kn_d = din("k_norm", [64])
    relb = din("rel_bias", [32, 8]); sinks_d = din("sinks", [8]); w_ao = din("w_attn_out", [512, D])
    wdw = din("w_dw", [31, 512]); bdw = din("b_dw", [512]); lng = din("conv_ln_g", [512]); lnb = din("conv_ln_b", [512])
    w_co = din("w_conv_out", [512, D]); w_o = din("w_out", [D, D])
    g2 = din("ffn2_norm", [D]); w21 = din("ffn2_w1", [D, DFF]); w23 = din("ffn2_w3", [D, DFF]); w22 = din("ffn2_w2", [DFF, D])
    identb_d = din("identb", [128, 128], BF16); identf_d = din("identf", [128, 128]); er_d = din("er", [33, 383])
    sel_d = din("sel", [120, 4])

    y_m = dout("y_m", [2048, D]); y_s = dout("y_s", [16, D])
    nkp = dout("nkp", [128, 128]); nvp = dout("nvp", [128, 128]); ncp = dout("ncp", [30, 512])
    nks = dout("nks", [16, 128, 128]); nvs = dout("nvs", [16, 128, 128]); ncs = dout("ncs", [16, 30, 512])

    if DEBUG_OUT:
        dbg1 = dout("dbg1", [2048, D]); dbg2 = dout("dbg2", [2048, D]); dbg3 = dout("dbg3", [2048, D])

    gscr = nc.dram_tensor("gscr", [8, 384], F32, kind="Internal").ap()

    es = contextlib.ExitStack()
    with es:
        def sb(name, shape, dt=F32):
            return es.enter_context(nc.sbuf_tensor(name, list(shape), dt))

        H = sb("H", [128, 5, D])
        UT = sb("UT", [128, 8, 640], BF16)
        ring = sb("ring", [128, NS, 3072], BF16)
        actb = [sb("act%d" % i, [128, GS, 640], BF16) for i in range(2)]
        stmp = [sb("stmp%d" % i, [128, 512]) for i in range(2)]
        xn = sb("xn", [128, D], BF16)
        junk = sb("junk", [128, D], BF16)
        biasT = sb("biasT", [128, 2, 8, 128])
        identb = sb("identb_s", [128, 128], BF16)
        identf = sb("identf_s", [128, 128])
        onesb = sb("onesb", [128, 64], BF16)
        onesf = sb("onesf", [128, 128])
        gT = sb("gT", [128, 3, 8])
        epsc = sb("epsc", [128, 1])
        zcol = sb("zcol", [128, 1])
        hmask = sb("hmask_s", [128, 1])
        ssq = sb("ssq", [128, 8])
        rstd = sb("rstd", [128, 8])
        qg = sb("qg", [128, 8, 64])
        kg = sb("kg", [128, 2, 64])
        esink = sb("esink", [128, 4])
        wdwT = sb("wdwT", [128, 4, 31])
        cvec = sb("cvec", [128, 3, 4])
        er = sb("er_s", [33, 383])
        rbx = sb("rbx", [33, 8])
        kT = sb("kT", [128, 768], BF16)
        Vb = sb("Vb", [128, 6, 128], BF16)
        gluT = sb("gluT", [128, 4, 670], BF16)
        glu32 = sb("glu32", [128, 4, 46])
        qkv = sb("qkv", [128, 768])
        sq = sb("sq", [128, 640])
        ssh = sb("ssh", [128, 10])
        rsh = sb("rsh", [128, 10])
        qtmp = sb("qtmp", [128, 512])
        qb = sb("qb", [128, 512], BF16)
        kf = sb("kf", [128, 128])
        kb = sb("kb", [128, 128], BF16)
        qT = sb("qT", [128, 4, 640], BF16)
        aT = sb("aT", [128, 4, 640], BF16)
        sc = sb("sc", [128, 1024])
        PT = sb("PT", [128, 2, 1024], BF16)
        dtmp = sb("dtmp", [128, 512])
        diags = [sb("diag%d" % i, [128, 31, 128], BF16) for i in range(2)]
        yb = sb("yb", [128, 4, 528])
        ysq = [sb("ysq%d" % i, [128, 512]) for i in range(2)]
        mean = sb("mean", [128, 528])
        var = sb("var", [128, 528])
        sw = sb("sw", [128, 4, 528], BF16)
        gA = sb("gA", [128, 512]); gC = sb("gC", [128, 512])
        mT = sb("mT", [128, 8, 528], BF16)
        Ks = sb("Ks", [128, 16, 128], BF16)
        Vs = sb("Vs", [128, 16, 128], BF16)
        KTs = sb("KTs", [128, 16, 128], BF16)
        st = sb("st", [120, 512])
        wrep = sb("wrep", [120, 512])
        sel = sb("sel_s", [120, 4])
        pst = sb("pst", [8, 6, 128])
        wdw_s = sb("wdw_s", [31, 512])
        otok = sb("otok", [128, 512])
        gs = sb("gs", [8, 384])
        ps = [es.enter_context(nc.psum_tensor("ps%d" % i, [128, 512], F32)) for i in range(8)]

        P = Prog(nc)
        bank_rr = [0]

        def bank():
            i = bank_rr[0]
            bank_rr[0] = (i + 1) % 8
            return i

        pieces = []

        def ffn_pieces(w1, w3, w2):
            w1v = w1.rearrange("(k p) n -> p k n", p=128)
            w3v = w3.rearrange("(k p) n -> p k n", p=128)
            out = []
            for c in range(NCH):
                out.append([
                    (lambda s, c=c: ring[:, s, 0:1024].rearrange("p (k n) -> p k n", k=8), w1v[:, :, c * 128:(c + 1) * 128]),
                    (lambda s, c=c: ring[:, s, 1024:2048].rearrange("p (k n) -> p k n", k=8), w3v[:, :, c * 128:(c + 1) * 128]),
                    (lambda s, c=c: ring[:, s, 2048:3072], w2[c * 128:(c + 1) * 128, :]),
                ])
            return out

        w_inv = w_in.rearrange("(k p) n -> p k n", p=128)
        w_aov = [w_ao[0:256, :].rearrange("(j p) n -> p j n", p=64), w_ao[256:512, :].rearrange("(j p) n -> p j n", p=64)]
        w_cov = w_co.rearrange("(k p) n -> p k n", p=128)
        w_ov = w_o.rearrange("(k p) n -> p k n", p=128)

        def mixer_pieces():
            out = []
            for q in range(2):
                out.append([(lambda s: ring[:, s, :].rearrange("p (k n) -> p k n", k=8), w_inv[:, :, q * 384:(q + 1) * 384])])
            for c in range(4):
                out.append([
                    (lambda s: ring[:, s, 0:1024].rearrange("p (k n) -> p k n", k=8), w_inv[:, :, 768 + c * 128:768 + (c + 1) * 128]),
                    (lambda s: ring[:, s, 1024:2048].rearrange("p (k n) -> p k n", k=8), w_inv[:, :, 1280 + c * 128:1280 + (c + 1) * 128]),
                ])
            for i in range(8):
                out.append([
                    (lambda s: ring[:, s, 0:1024].rearrange("p (k n) -> p k n", k=8), w_inv[:, :, 1792 + i * 128:1792 + (i + 1) * 128]),
                    (lambda s: ring[:, s, 1024:2048].rearrange("p (k n) -> p k n", k=8), w_inv[:, :, 2816 + i * 128:2816 + (i + 1) * 128]),
                    (lambda s: ring[0:64, s, 2048:2560].rearrange("p (j n) -> p j n", j=4), w_aov[0][:, :, i * 128:(i + 1) * 128]),
                    (lambda s: ring[64:128, s, 2048:2560].rearrange("p (j n) -> p j n", j=4), w_aov[1][:, :, i * 128:(i + 1) * 128]),
                    (lambda s: ring[:, s, 2560:3072].rearrange("p (k n) -> p k n", k=4), w_cov[:, :, i * 128:(i + 1) * 128]),
                ])
            for (a, b) in ((0, 384), (384, 768), (768, 1024)):
                out.append([(lambda s, a=a, b=b: ring[:, s, 0:8 * (b - a)].rearrange("p (k n) -> p k n", k=8), w_ov[:, :, a:b])])
            return out

        for t in range(4):
            pieces += ffn_pieces(w11, w13, w12)
            pieces += mixer_pieces()
            pieces += ffn_pieces(w21, w23, w22)
        n_pieces = len(pieces)
        loaded = [0]

        def w_load(k):
            if k >= n_pieces:
                return
            assert k == loaded[0]
            s = k % NS
            for j, (dstf, src) in enumerate(pieces[k]):
                dst = dstf(s)
                P.dma("pool", lambda e, dst=dst, src=src: e.dma_start(out=dst, in_=src), writes=[("ring", s, j)])
            loaded[0] = k + 1

        cur = [0]

        def RK(s):
            return [("ring", s, j) for j in range(5)]

        def w_use():
            k = cur[0]
            assert k < loaded[0], "piece not loaded"
            cur[0] += 1
            return k, k % NS

        def w_release(k):
            w_load(k + NS)

        def ld(dst, src, key, q="sync"):
            P.dma(q, lambda e: e.dma_start(out=dst, in_=src), writes=[key])

        with nc.allow_non_contiguous_dma(reason="tiny one-time parameter loads"):
            for k in range(NS):
                w_load(k)
            ld(identb[:], identb_d, "identb"); ld(identf[:], identf_d, "identf")
            ld(er[:], er_d, "er"); ld(sel[:], sel_d, "sel"); ld(hmask[:], hmask_d, "hmask")
            ld(pst[0:8, 0, :], g1.rearrange("(k p) -> k p", p=128), ("pst", 0))
            ld(pst[0:8, 1, :], gm.rearrange("(k p) -> k p", p=128), ("pst", 1))
            ld(pst[0:8, 2, :], g2.rearrange("(k p) -> k p", p=128), ("pst", 2))
            ld(pst[0:4, 3, :], bdw.rearrange("(k p) -> k p", p=128), ("pst", 3))
            ld(pst[0:4, 4, :], lng.rearrange("(k p) -> k p", p=128), ("pst", 4))
            ld(pst[0:4, 5, :], lnb.rearrange("(k p) -> k p", p=128), ("pst", 5))
            ld(wdw_s[:, :], wdw, "wdw_s")
            ld(qg[:, 0, :], qn_d.partition_broadcast(128), "qg0")
            ld(kg[:, 0, :], kn_d.partition_broadcast(128), "kg0")
            ld(esink[0:64, :], sinks_d[0:4].partition_broadcast(64), "esink")
            ld(esink[64:128, :], sinks_d[4:8].partition_broadcast(64), "esink2")
            ld(rbx[0:32, :], relb, "rbx")
            for a in range(4):
                ld(wrep[a * 30:(a + 1) * 30, :], wdw[0:30, :], ("wrep", a))
        def tsmall(dst, src, npart, rkeys, wkey):
            b = bank()
            P.pe(lambda e, b=b: ST(e.transpose(ps[b][:, 0:npart], src, identf[0:npart, 0:npart]), *rkeys), reads=rkeys + ["identf"], writes=[("ps", b)])
            P.dve(lambda e, b=b: e.tensor_copy(out=dst, in_=ps[b][:, 0:npart]), reads=[("ps", b)], writes=[wkey])
        for i in range(3):
            tsmall(gT[:, i, :], pst[0:8, i, :], 8, [("pst", i)], "gT%d" % i)
            tsmall(cvec[:, i, :], pst[0:4, 3 + i, :], 4, [("pst", 3 + i)], "cv%d" % i)
        for c in range(4):
            tsmall(wdwT[:, c, :], wdw_s[0:31, c * 128:(c + 1) * 128], 31, ["wdw_s"], "wdwT")
        P.dve(lambda e: e.memset(onesb[:], 1.0), writes=["onesb"])
        P.dve(lambda e: e.memset(onesf[:], 1.0), writes=["onesf"])
        P.dve(lambda e: e.memset(epsc[:], EPS), writes=["epsc"])
        P.dve(lambda e: e.memset(zcol[:], 0.0), writes=["zcol"])
        P.dve(lambda e: e.memset(rbx[32:33, :], NEG), writes=["rbx2"])
        P.dve(lambda e: e.memset(gluT[:, :, 0:30], 0.0), writes=[("gluc", c) for c in range(4)])
        P.dve(lambda e: e.tensor_scalar(out=qg[:, 0, :], in0=qg[:, 0, :], scalar1=0.125, scalar2=None, op0=ALU.mult),
              reads=["qg0"], writes=["qg0"])
        for hh in range(1, 8):
            P.dve(lambda e, hh=hh: e.tensor_copy(out=qg[:, hh, :], in_=qg[:, 0, :]), reads=["qg0"], writes=[("qg", hh)])
        P.dve(lambda e: e.tensor_copy(out=kg[:, 1, :], in_=kg[:, 0, :]), reads=["kg0"], writes=["kg1"])
        QG = ["qg0"] + [("qg", hh) for hh in range(1, 8)]
        P.act(lambda e: e.activation(out=esink[:], in_=esink[:], func=AF.Exp), reads=["esink", "esink2"], writes=["esink", "esink2"])
        def build_bias():
            b = bank()
            P.pe(lambda e, b=b: e.matmul(ps[b][0:8, 0:383], lhsT=rbx[:, :], rhs=er[:, 0:383], start=True, stop=True),
                 reads=["er", "rbx", "rbx2"], writes=[("ps", b)])
            P.dve(lambda e, b=b: e.tensor_copy(out=gs[:, 0:383], in_=ps[b][0:8, 0:383]), reads=[("ps", b)], writes=["gs"])
            P.dma("sync", lambda e: e.dma_start(out=gscr[:, 0:383], in_=gs[:, 0:383]), reads=["gs"], writes=["gscr"])
            for j in range(128):
                for c in range(2):
                    off = 127 - j + 128 * (1 - c)
                    P.dma("sync", lambda e, j=j, c=c, off=off: e.dma_start(out=biasT[j:j + 1, c, :, :], in_=gscr[:, off:off + 128].unsqueeze(0)),
                          reads=["gscr"], writes=[("biasT", c, j)])
        BIAS = [("biasT", c, j) for c in range(2) for j in range(128)]

        tiles = _tiles()
        if DEBUG_OUT:
            P.dve(lambda e: e.memset(kT[:, :], 0.0), writes=[("kT", i) for i in range(6)])
            P.dve(lambda e: e.memset(Vb[:, :, :], 0.0), writes=[("V", i) for i in range(6)])
            P.dve(lambda e: e.memset(qT[:, :, :], 0.0), writes=[("qT", c) for c in (0, 128, 256, 384, 512)])
            P.dve(lambda e: e.memset(aT[:, :, :], 0.0), writes=[("aT", c) for c in (0, 128, 256, 384, 512)])
            P.dve(lambda e: e.memset(gluT[:, :, :], 0.0), writes=[("glu", c) for c in range(4)] + [("gluc", c) for c in range(4)])
            P.dve(lambda e: e.memset(sw[:, :, :], 0.0), writes=[("sw", c, o) for c in range(4) for o in (0, 512)])
            P.dve(lambda e: e.memset(mT[:, :, :], 0.0), writes=[("mT", i, o) for i in range(8) for o in (0, 512)])

        def norm_stage(T, gi, subs):
            n = len(subs)
            P.dve(lambda e: e.memset(ssq[:, :], 0.0), writes=[("ssq", i) for i in range(8)])
            for i, s in enumerate(subs):
                P.act(lambda e, s=s, i=i: e.activation(out=junk[0:s["n"], :], in_=H[0:s["n"], s["h"], :], func=AF.Square,
                                                       accum_out=ssq[0:s["n"], i:i + 1]),
                      reads=[("H", s["h"])], writes=["junk", ("ssq", i)])
            P.act(lambda e: e.activation(out=rstd[:, 0:n], in_=ssq[:, 0:n], func=AF.Sqrt, bias=epsc[:, 0:1], scale=1.0 / D),
                  reads=[("ssq", i) for i in range(n)] + ["epsc"], writes=["rstd"])
            P.dve(lambda e: e.reciprocal(out=rstd[:, 0:n], in_=rstd[:, 0:n]), reads=["rstd"], writes=["rstd"])
            for i, s in enumerate(subs):
                nt = s["n"]
                P.act(lambda e, s=s, i=i, nt=nt: e.activation(out=xn[0:nt, :], in_=H[0:nt, s["h"], :], func=AF.Copy,
                                                              scale=rstd[0:nt, i:i + 1]),
                      reads=[("H", s["h"]), "rstd"], writes=["xn"])
                b = bank()
                pb = ps[b][:, :].bitcast(BF16)

                def fn(e, nt=nt, pb=pb):
                    ins = None
                    for k in range(8):
                        ins = ST(e.transpose(pb[:, k * 128:k * 128 + nt], xn[0:nt, k * 128:(k + 1) * 128], identb[0:nt, 0:nt]), "xn")
                    return ins
                P.pe(fn, reads=["xn", "identb"], writes=[("ps", b)])
                P.dve(lambda e, s=s, nt=nt, pb=pb: e.tensor_tensor(
                    out=UT[:, :, s["c0"]:s["c0"] + nt],
                    in0=pb[:, 0:1024].rearrange("p (k n) -> p k n", k=8)[:, :, 0:nt],
                    in1=gT[:, gi, :].unsqueeze(2).to_broadcast([128, 8, nt]), op=ALU.mult),
                    reads=[("ps", b), "gT%d" % gi], writes=[("UT", s["c0"])])

        def ut_keys(T, cg):
            return [("UT", s["c0"]) for s in T["subs"] if cg[0] <= s["c0"] < cg[0] + cg[1]]

        def ffn_stage(T, cgs, subs):
            groups = [list(range(g, min(g + GS, NCH))) for g in range(0, NCH, GS)]
            for gi_, grp in enumerate(groups):
                ab = actb[gi_ % 2]
                used = []
                for ci, c in enumerate(grp):
                    k, s = w_use()
                    used.append((k, s))
                    w1c = ring[:, s, 0:1024].rearrange("p (k n) -> p k n", k=8)
                    w3c = ring[:, s, 1024:2048].rearrange("p (k n) -> p k n", k=8)
                    for cg in cgs:
                        c0, n = cg
                        b1 = bank(); b3 = bank()

                        def fn(e, b1=b1, b3=b3, c0=c0, n=n, w1c=w1c, w3c=w3c, s=s):
                            ins = None
                            for kk in range(8):
                                ins = ST(e.matmul(ps[b1][:, 0:n], lhsT=w1c[:, kk, :], rhs=UT[:, kk, c0:c0 + n], start=(kk == 0), stop=(kk == 7)), ("ring", s, 0))
                            for kk in range(8):
                                ins = ST(e.matmul(ps[b3][:, 0:n], lhsT=w3c[:, kk, :], rhs=UT[:, kk, c0:c0 + n], start=(kk == 0), stop=(kk == 7)), ("ring", s, 1))
                            return ins
                        P.pe(fn, reads=RK(s) + ut_keys(T, cg), writes=[("ps", b1), ("ps", b3)])
                        tb = stmp[(ci + (0 if cg is cgs[0] else 1)) % 2]
                        tk = "stmp%d" % ((ci + (0 if cg is cgs[0] else 1)) % 2)
                        P.act(lambda e, b1=b1, n=n, tb=tb: e.activation(out=tb[:, 0:n], in_=ps[b1][:, 0:n], func=AF.Silu),
                              reads=[("ps", b1)], writes=[tk])
                        P.dve(lambda e, b3=b3, n=n, tb=tb, ab=ab, ci=ci, c0=c0: e.tensor_tensor(
                            out=ab[:, ci, c0:c0 + n], in0=tb[:, 0:n], in1=ps[b3][:, 0:n], op=ALU.mult),
                            reads=[tk, ("ps", b3)], writes=[("act", gi_ % 2, ci, c0)])
                for sidx, sub in enumerate(subs):
                    nt = sub["n"]
                    cgk = [cg for cg in cgs if cg[0] <= sub["c0"] < cg[0] + cg[1]][0]
                    for half in range(2):
                        b = bank()

                        def fn(e, b=b, nt=nt, sub=sub, half=half, used=used, ab=ab, gi_=gi_, cgk=cgk):
                            ins = None
                            for ci, (k, s) in enumerate(used):
                                ins = ST(e.matmul(ps[b][0:nt, :], lhsT=ab[:, ci, sub["c0"]:sub["c0"] + nt],
                                                  rhs=ring[:, s, 2048 + half * 512:2048 + (half + 1) * 512],
                                                  start=(ci == 0), stop=(ci == len(used) - 1)), ("act", gi_ % 2, ci, cgk[0]))
                            return ins
                        P.pe(fn, reads=[kk_ for (k, s) in used for kk_ in RK(s)] + [("act", gi_ % 2, ci, cgk[0]) for ci in range(len(used))],
                             writes=[("ps", b)])
                        P.dve(lambda e, b=b, nt=nt, sub=sub, half=half: e.scalar_tensor_tensor(
                            out=H[0:nt, sub["h"], half * 512:(half + 1) * 512], in0=ps[b][0:nt, :], scalar=0.5,
                            in1=H[0:nt, sub["h"], half * 512:(half + 1) * 512], op0=ALU.mult, op1=ALU.add),
                            reads=[("ps", b), ("H", sub["h"])], writes=[("H", sub["h"])])
                for (k, s) in used:
                    w_release(k)

        def qkv_stage(T):
            k0, s0 = w_use()
            k1, s1 = w_use()
            wa = ring[:, s0, :].rearrange("p (k n) -> p k n", k=8)
            wb = ring[:, s1, :].rearrange("p (k n) -> p k n", k=8)
            last = T["t"] == 3
            for sub in T["subs"]:
                nt = sub["n"]; c0 = sub["c0"]
                ba = bank(); bb = bank()

                def fn(e, ba=ba, bb=bb, nt=nt, c0=c0):
                    ins = None
                    for kk in range(8):
                        ins = ST(e.matmul(ps[ba][0:nt, 0:384], lhsT=UT[:, kk, c0:c0 + nt], rhs=wa[:, kk, :], start=(kk == 0), stop=(kk == 7)), ("UT", c0))
                    for kk in range(8):
                        ins = ST(e.matmul(ps[bb][0:nt, 0:384], lhsT=UT[:, kk, c0:c0 + nt], rhs=wb[:, kk, :], start=(kk == 0), stop=(kk == 7)), ("UT", c0))
                    return ins
                P.pe(fn, reads=RK(s0) + RK(s1) + [("UT", c0)], writes=[("ps", ba), ("ps", bb)])
                P.act(lambda e, ba=ba, nt=nt: e.activation(out=qkv[0:nt, 0:384], in_=ps[ba][0:nt, 0:384], func=AF.Copy),
                      reads=[("ps", ba)], writes=["qkvA"])
                P.dve(lambda e, bb=bb, nt=nt: e.tensor_copy(out=qkv[0:nt, 384:768], in_=ps[bb][0:nt, 0:384]),
                      reads=[("ps", bb)], writes=["qkvB"])
                P.act(lambda e, nt=nt: e.activation(out=sq[0:nt, :], in_=qkv[0:nt, 0:640], func=AF.Square),
                      reads=["qkvA", "qkvB"], writes=["sq"])
                P.dve(lambda e, nt=nt: e.tensor_reduce(out=ssh[0:nt, :], in_=sq[0:nt, :].rearrange("p (h d) -> p h d", d=64),
                                                       axis=AX.X, op=ALU.add), reads=["sq"], writes=["ssh"])
                P.act(lambda e, nt=nt: e.activation(out=rsh[0:nt, :], in_=ssh[0:nt, :], func=AF.Sqrt, bias=epsc[0:nt, 0:1], scale=1.0 / 64),
                      reads=["ssh", "epsc"], writes=["rsh"])
                P.dve(lambda e, nt=nt: e.reciprocal(out=rsh[0:nt, :], in_=rsh[0:nt, :]), reads=["rsh"], writes=["rsh"])
                P.dve(lambda e, nt=nt: e.tensor_tensor(
                    out=qtmp[0:nt, :].rearrange("p (h d) -> p h d", d=64), in0=qkv[0:nt, 0:512].rearrange("p (h d) -> p h d", d=64),
                    in1=rsh[0:nt, 0:8].unsqueeze(2).to_broadcast([nt, 8, 64]), op=ALU.mult),
                    reads=["qkvA", "qkvB", "rsh"], writes=["qtmp"])
                P.dve(lambda e, nt=nt: e.tensor_tensor(
                    out=qb[0:nt, :].rearrange("p (j l d) -> p l j d", j=4, l=2),
                    in0=qtmp[0:nt, :].rearrange("p (l j d) -> p l j d", l=2, j=4),
                    in1=qg[0:nt, :, :].rearrange("p (l j) d -> p l j d", l=2), op=ALU.mult),
                    reads=["qtmp"] + QG, writes=["qb"])
                P.dve(lambda e, nt=nt: e.tensor_tensor(
                    out=kf[0:nt, :].rearrange("p (h d) -> p h d", d=64), in0=qkv[0:nt, 512:640].rearrange("p (h d) -> p h d", d=64),
                    in1=rsh[0:nt, 8:10].unsqueeze(2).to_broadcast([nt, 2, 64]), op=ALU.mult),
                    reads=["qkvB", "rsh"], writes=["kf"])
                P.dve(lambda e, nt=nt: e.tensor_tensor(
                    out=kf[0:nt, :].rearrange("p (h d) -> p h d", d=64), in0=kf[0:nt, :].rearrange("p (h d) -> p h d", d=64),
                    in1=kg[0:nt, :, :], op=ALU.mult), reads=["kf", "kg0", "kg1"], writes=["kf"])
                P.act(lambda e, nt=nt: e.activation(out=kb[0:nt, :], in_=kf[0:nt, :], func=AF.Copy), reads=["kf"], writes=["kb"])
                vslot = 1 + c0 // 128
                if sub["kind"] != "samp":
                    P.act(lambda e, vslot=vslot: e.activation(out=Vb[:, vslot, :], in_=qkv[:, 640:768], func=AF.Copy),
                          reads=["qkvB"], writes=[("V", vslot)])
                bq = bank()
                pq = ps[bq][:, :].bitcast(BF16)

                def fn(e, nt=nt, pq=pq):
                    ins = None
                    for j in range(4):
                        ins = ST(e.transpose(pq[:, j * 128:j * 128 + nt], qb[0:nt, j * 128:(j + 1) * 128], identb[0:nt, 0:nt]), "qb")
                    ins = ST(e.transpose(pq[:, 512:512 + nt], kb[0:nt, :], identb[0:nt, 0:nt]), "kb")
                    return ins
                P.pe(fn, reads=["qb", "kb", "identb"], writes=[("ps", bq)])
                P.act(lambda e, nt=nt, c0=c0, pq=pq: e.activation(
                    out=qT[:, :, c0:c0 + nt], in_=pq[:, 0:512].rearrange("p (j n) -> p j n", j=4)[:, :, 0:nt], func=AF.Copy),
                    reads=[("ps", bq)], writes=[("qT", c0)])
                if sub["kind"] != "samp":
                    P.act(lambda e, nt=nt, c0=c0, pq=pq: e.activation(out=kT[:, 128 + c0:128 + c0 + nt], in_=pq[:, 512:512 + nt], func=AF.Copy),
                          reads=[("ps", bq)], writes=[("kT", vslot)])
                if last and sub["kind"] == "main" and sub["blk"] == 15:
                    P.dma("sync", lambda e: e.dma_start(out=nkp, in_=kf[:, :]), reads=["kf"], writes=["o_nkp"])
                    P.dma("sync", lambda e: e.dma_start(out=nvp, in_=qkv[:, 640:768]), reads=["qkvB"], writes=["o_nvp"])
                if sub["kind"] == "samp":
                    P.dma("sync", lambda e: e.dma_start(out=nks[:, 127, :], in_=kf[0:16, :]), reads=["kf"], writes=["o_nks_new"])
                    P.dma("sync", lambda e: e.dma_start(out=nvs[:, 127, :], in_=qkv[0:16, 640:768]), reads=["qkvB"], writes=["o_nvs_new"])
            w_release(k0); w_release(k1)

        def glu_stage(T):
            last = T["t"] == 3
            for c in range(4):
                k, s = w_use()
                wa = ring[:, s, 0:1024].rearrange("p (k n) -> p k n", k=8)
                wb = ring[:, s, 1024:2048].rearrange("p (k n) -> p k n", k=8)
                for cg in T["cgs_all"]:
                    c0, n = cg
                    ba = bank(); bb = bank()

                    def fn(e, ba=ba, bb=bb, c0=c0, n=n, wa=wa, wb=wb, s=s):
                        ins = None
                        for kk in range(8):
                            ins = ST(e.matmul(ps[ba][:, 0:n], lhsT=wa[:, kk, :], rhs=UT[:, kk, c0:c0 + n], start=(kk == 0), stop=(kk == 7)), ("ring", s, 0))
                        for kk in range(8):
                            ins = ST(e.matmul(ps[bb][:, 0:n], lhsT=wb[:, kk, :], rhs=UT[:, kk, c0:c0 + n], start=(kk == 0), stop=(kk == 7)), ("ring", s, 1))
                        return ins
                    P.pe(fn, reads=RK(s) + ut_keys(T, cg), writes=[("ps", ba), ("ps", bb)])
                    P.act(lambda e, bb=bb, n=n: e.activation(out=gA[:, 0:n], in_=ps[bb][:, 0:n], func=AF.Sigmoid),
                          reads=[("ps", bb)], writes=["gA"])
                    P.dve(lambda e, ba=ba, n=n, c0=c0, c=c: e.tensor_tensor(
                        out=gluT[:, c, 30 + c0:30 + c0 + n], in0=gA[:, 0:n], in1=ps[ba][:, 0:n], op=ALU.mult),
                        reads=["gA", ("ps", ba)], writes=[("glu", c)])
                    if last:
                        if n == 512:
                            P.dve(lambda e, ba=ba, c=c: e.tensor_tensor(
                                out=glu32[:, c, 0:30], in0=gA[:, 482:512], in1=ps[ba][:, 482:512], op=ALU.mult),
                                reads=["gA", ("ps", ba)], writes=[("glu32m", c)])
                        else:
                            P.dve(lambda e, ba=ba, c=c: e.tensor_tensor(
                                out=glu32[:, c, 30:46], in0=gA[:, 0:16], in1=ps[ba][:, 0:16], op=ALU.mult),
                                reads=["gA", ("ps", ba)], writes=[("glu32s", c)])
                w_release(k)

        def attn_stage(T):
            first_core_blk = True
            for sub in T["subs"]:
                if sub["kind"] != "main":
                    continue
                c0 = sub["c0"]
                pslot = c0 // 128
                oslot = 1 + c0 // 128
                use_mask = (sub["blk"] == 0)
                for c, slot in enumerate((pslot, oslot)):
                    b0 = bank(); b1 = bank()

                    def fn(e, b0=b0, b1=b1, slot=slot, c0=c0):
                        ST(e.matmul(ps[b0][:, :], lhsT=kT[0:64, slot * 128:(slot + 1) * 128], rhs=qT[0:64, :, c0:c0 + 128], start=True, stop=True), ("kT", slot))
                        return ST(e.matmul(ps[b1][:, :], lhsT=kT[64:128, slot * 128:(slot + 1) * 128], rhs=qT[64:128, :, c0:c0 + 128],
                                           start=True, stop=True), ("kT", slot))
                    P.pe(fn, reads=[("kT", slot), ("qT", c0)], writes=[("ps", b0), ("ps", b1)])
                    mcol = hmask if (use_mask and c == 0) else zcol
                    for hf, b in enumerate((b0, b1)):
                        P.dve(lambda e, b=b, hf=hf, c=c, mcol=mcol: e.scalar_tensor_tensor(
                            out=sc[:, hf * 512:(hf + 1) * 512], in0=ps[b][:, :], scalar=mcol[:, 0:1],
                            in1=biasT[:, c, hf * 4:(hf + 1) * 4, :].rearrange("p h q -> p (h q)"), op0=ALU.add, op1=ALU.add),
                            reads=[("ps", b), "hmask", "zcol"] + BIAS, writes=[("sc", hf)])
                    P.act(lambda e, c=c: e.activation(out=PT[:, c, :], in_=sc[:, :], func=AF.Exp),
                          reads=[("sc", 0), ("sc", 1)], writes=[("PT", c)])
                bo = bank(); bd = bank()

                def fn(e, bo=bo, bd=bd, pslot=pslot, oslot=oslot):
                    ins = None
                    for g in range(2):
                        for c, slot in enumerate((pslot, oslot)):
                            ins = ST(e.matmul(ps[bo][g * 64:(g + 1) * 64, :], lhsT=Vb[:, slot, g * 64:(g + 1) * 64],
                                              rhs=PT[:, c, g * 512:(g + 1) * 512], start=(c == 0), stop=(c == 1),
                                              tile_position=(0, g * 64)), ("V", slot))
                    for g in range(2):
                        for c in range(2):
                            ins = ST(e.matmul(ps[bd][g * 64:(g + 1) * 64, :], lhsT=onesb[:, :],
                                              rhs=PT[:, c, g * 512:(g + 1) * 512], start=(c == 0), stop=(c == 1),
                                              tile_position=(0, g * 64)), "onesb")
                    return ins
                P.pe(fn, reads=[("V", pslot), ("V", oslot), ("PT", 0), ("PT", 1), "onesb"], writes=[("ps", bo), ("ps", bd)])
                P.dve(lambda e, bd=bd: e.tensor_tensor(
                    out=dtmp[:, :].rearrange("p (j q) -> p j q", j=4), in0=ps[bd][:, :].rearrange("p (j q) -> p j q", j=4),
                    in1=esink[:, :].unsqueeze(2).to_broadcast([128, 4, 128]), op=ALU.add),
                    reads=[("ps", bd), "esink", "esink2"], writes=["dtmp"])
                P.dve(lambda e: e.reciprocal(out=dtmp[:, :], in_=dtmp[:, :]), reads=["dtmp"], writes=["dtmp"])
                P.dve(lambda e, bo=bo, c0=c0: e.tensor_tensor(
                    out=aT[:, :, c0:c0 + 128], in0=ps[bo][:, :].rearrange("p (j q) -> p j q", j=4),
                    in1=dtmp[:, :].rearrange("p (j q) -> p j q", j=4), op=ALU.mult),
                    reads=[("ps", bo), "dtmp"], writes=[("aT", c0)])
            lm = [s for s in T["subs"] if s["kind"] == "main"][-1]
            ls = 1 + lm["c0"] // 128
            if T["t"] < 3:
                P.act(lambda e, ls=ls: e.activation(out=kT[:, 0:128], in_=kT[:, ls * 128:(ls + 1) * 128], func=AF.Copy),
                      reads=[("kT", ls)], writes=[("kT", 0)])
                P.act(lambda e, ls=ls: e.activation(out=Vb[:, 0, :], in_=Vb[:, ls, :], func=AF.Copy),
                      reads=[("V", ls)], writes=[("V", 0)])

        def sample_attn():
            SC0 = 512
            P.dma("sync", lambda e: e.dma_start(out=nks[:, 0:127, :], in_=ck[:, 1:128, :]), writes=["o_nks_old"])
            P.dma("sync", lambda e: e.dma_start(out=nvs[:, 0:127, :], in_=cv[:, 1:128, :]), writes=["o_nvs_old"])
            P.dma("pool", lambda e: e.dma_start(out=Ks[:, :, :], in_=nks.rearrange("i j d -> j i d")),
                  reads=["o_nks_old", "o_nks_new"], writes=["Ks"])
            P.dma("pool", lambda e: e.dma_start(out=Vs[:, :, :], in_=nvs.rearrange("i j d -> j i d")),
                  reads=["o_nvs_old", "o_nvs_new"], writes=["Vs"])
            if SUBCUT < 2:
                return
            for g4 in range(4):
                b = bank()
                pb = ps[b][:, :].bitcast(BF16)

                def fn(e, g4=g4, pb=pb):
                    ins = None
                    for ii in range(4):
                        ins = ST(e.transpose(pb[:, ii * 128:(ii + 1) * 128], Ks[:, g4 * 4 + ii, :], identb[:, :]), "Ks")
                    return ins
                P.pe(fn, reads=["Ks", "identb"], writes=[("ps", b)])
                P.act(lambda e, g4=g4, pb=pb: e.activation(out=KTs[:, g4 * 4:(g4 + 1) * 4, :],
                                                           in_=pb[:, 0:512].rearrange("p (i n) -> p i n", i=4), func=AF.Copy),
                      reads=[("ps", b)], writes=[("KTs", g4)])
            if SUBCUT < 3:
                return
            bA = bank(); bB = bank()

            def fn(e, bA=bA, bB=bB):
                ins = None
                for i in range(16):
                    e.matmul(ps[bA][:, i * 4:i * 4 + 4], lhsT=KTs[0:64, i, :], rhs=qT[0:64, :, SC0 + i], start=True, stop=True)
                    ins = e.matmul(ps[bB][:, i * 4:i * 4 + 4], lhsT=KTs[64:128, i, :], rhs=qT[64:128, :, SC0 + i], start=True, stop=True)
                return ins
            P.pe(fn, reads=[("KTs", g4) for g4 in range(4)] + [("qT", SC0)], writes=[("ps", bA), ("ps", bB)])
            for l, bX in enumerate((bA, bB)):
                P.dve(lambda e, l=l, bX=bX: e.tensor_tensor(
                    out=sc[:, 0:128].rearrange("p (i h) -> p i h", h=8)[:, :, l * 4:(l + 1) * 4],
                    in0=ps[bX][:, 0:64].rearrange("p (i j) -> p i j", j=4),
                    in1=biasT[:, 1, l * 4:(l + 1) * 4, 127].unsqueeze(1).to_broadcast([128, 16, 4]), op=ALU.add),
                    reads=[("ps", bX)] + BIAS, writes=[("sc", l)])
            P.act(lambda e: e.activation(out=PT[:, 0, 0:128], in_=sc[:, 0:128], func=AF.Exp), reads=[("sc", 0), ("sc", 1)], writes=[("PT", 0)])
            if SUBCUT < 4:
                return
            bo = bank(); bd = bank()
            ptv = PT[:, 0, 0:128].rearrange("p (i l j) -> p i l j", l=2, j=4)

            def fn(e, bo=bo, bd=bd):
                ins = None
                for i in range(16):
                    for g in range(2):
                        ins = ST(e.matmul(ps[bo][g * 64:(g + 1) * 64, i * 4:(i + 1) * 4], lhsT=Vs[:, i, g * 64:(g + 1) * 64],
                                          rhs=ptv[:, i, g, :], start=True, stop=True, tile_position=(0, g * 64)), "Vs")
                for g in range(2):
                    ins = ST(e.matmul(ps[bd][g * 64:(g + 1) * 64, 0:64].rearrange("p (i j) -> p i j", j=4), lhsT=onesb[:, :],
                                      rhs=ptv[:, :, g, :], start=True, stop=True, tile_position=(0, g * 64)), "onesb")
                return ins
            P.pe(fn, reads=["Vs", ("PT", 0), "onesb"], writes=[("ps", bo), ("ps", bd)])
            if SUBCUT < 5:
                return
            P.dve(lambda e, bd=bd: e.tensor_tensor(
                out=dtmp[:, 0:64].rearrange("p (i j) -> p i j", j=4), in0=ps[bd][:, 0:64].rearrange("p (i j) -> p i j", j=4),
                in1=esink[:, :].unsqueeze(1).to_broadcast([128, 16, 4]), op=ALU.add),
                reads=[("ps", bd), "esink", "esink2"], writes=["dtmp"])
            P.dve(lambda e: e.reciprocal(out=dtmp[:, 0:64], in_=dtmp[:, 0:64]), reads=["dtmp"], writes=["dtmp"])
            P.dve(lambda e, bo=bo: e.tensor_tensor(
                out=aT[:, :, SC0:SC0 + 16], in0=ps[bo][:, 0:64].rearrange("p (i j) -> p j i", j=4),
                in1=dtmp[:, 0:64].rearrange("p (i j) -> p j i", j=4), op=ALU.mult),
                reads=[("ps", bo), "dtmp"], writes=[("aT", SC0)])

        def conv_stage(T):
            c0, n = T["main_cg"]
            last = T["t"] == 3
            gk = []
            for c in range(4):
                gk.append([("glu", c), ("gluc", c)])
            if last:
                bs = bank()
                for r in range(4):
                    P.dma("sync", lambda e, r=r: e.dma_start(out=st[:, :], in_=stc[r * 4:(r + 1) * 4, :, :].rearrange("a j c -> (a j) c")),
                          writes=["st"])
                    P.dve(lambda e: e.tensor_tensor(out=st[:, :], in0=st[:, :], in1=wrep[:, :], op=ALU.mult),
                          reads=["st"] + [("wrep", a) for a in range(4)], writes=["st"])

                    def fn(e, r=r, bs=bs):
                        ins = None
                        for c in range(4):
                            ins = ST(e.matmul(ps[bs][:, c * 16 + r * 4:c * 16 + r * 4 + 4], lhsT=st[:, c * 128:(c + 1) * 128], rhs=sel[:, :],
                                              start=True, stop=True), "st")
                        return ins
                    P.pe(fn, reads=["st", "sel"], writes=[("ps", bs)])
                P.dma("sync", lambda e: e.dma_start(out=ncs[:, 0:29, :], in_=stc[:, 1:30, :]), writes=["o_ncs_old"])
            for c in range(4):
                diag = diags[c % 2]
                P.dve(lambda e, c=c, diag=diag: e.tensor_tensor(
                    out=diag[:, :, :], in0=identf[:, :].unsqueeze(1).to_broadcast([128, 31, 128]),
                    in1=wdwT[:, c, :].unsqueeze(2).to_broadcast([128, 31, 128]), op=ALU.mult),
                    reads=["identf", "wdwT"], writes=["diag%d" % (c % 2)])
                b = bank()

                def fn(e, b=b, c=c, diag=diag):
                    ins = None
                    for j in range(31):
                        ins = e.matmul(ps[b][:, :], lhsT=diag[:, j, :], rhs=gluT[:, c, c0 + j:c0 + j + 512], start=(j == 0), stop=(j == 30))
                    return ins
                P.pe(fn, reads=["diag%d" % (c % 2)] + gk[c], writes=[("ps", b)])
                P.act(lambda e, b=b, c=c: e.activation(out=yb[:, c, 0:512], in_=ps[b][:, :], func=AF.Identity, bias=cvec[:, 0, c:c + 1]),
                      reads=[("ps", b), "cv0"], writes=[("y", c, 0)])
                if last:
                    P.dve(lambda e, c=c: e.scalar_tensor_tensor(
                        out=yb[:, c, 512:528], in0=glu32[:, c, 30:46], scalar=wdwT[:, c, 30:31], in1=ps[bs][:, c * 16:(c + 1) * 16],
                        op0=ALU.mult, op1=ALU.add), reads=[("glu32s", c), "wdwT", ("ps", bs)], writes=[("y", c, 1)])
                    P.dve(lambda e, c=c: e.tensor_scalar(out=yb[:, c, 512:528], in0=yb[:, c, 512:528], scalar1=cvec[:, 0, c:c + 1],
                                                         scalar2=None, op0=ALU.add), reads=[("y", c, 1), "cv0"], writes=[("y", c, 1)])
            ntot = 528 if last else 512
            YK = [("y", c, 0) for c in range(4)] + ([("y", c, 1) for c in range(4)] if last else [])
            b1 = bank(); b2 = bank()
            for c in range(4):
                yq = ysq[c % 2]
                P.act(lambda e, c=c, yq=yq: e.activation(out=yq[:, 0:ntot] if ntot <= 512 else yq[:, 0:512], in_=yb[:, c, 0:min(ntot, 512)], func=AF.Square),
                      reads=YK, writes=["ysq%d" % (c % 2)])

                def fn(e, c=c, yq=yq):
                    ST(e.matmul(ps[b1][:, :], lhsT=onesf[:, :], rhs=yb[:, c, 0:512], start=(c == 0), stop=(c == 3)), "onesf")
                    return ST(e.matmul(ps[b2][:, :], lhsT=onesf[:, :], rhs=yq[:, 0:512], start=(c == 0), stop=(c == 3)), "onesf")
                P.pe(fn, reads=YK + ["ysq%d" % (c % 2), "onesf"], writes=[("ps", b1), ("ps", b2)])
            cgl = [(0, 512, b1, b2)]
            if last:
                b3 = bank(); b4 = bank()
                for c in range(4):
                    yq = ysq[c % 2]
                    P.act(lambda e, c=c, yq=yq: e.activation(out=yq[:, 0:16], in_=yb[:, c, 512:528], func=AF.Square),
                          reads=YK, writes=["ysq%d" % (c % 2)])

                    def fn(e, c=c, yq=yq):
                        ST(e.matmul(ps[b3][:, 0:16], lhsT=onesf[:, :], rhs=yb[:, c, 512:528], start=(c == 0), stop=(c == 3)), "onesf")
                        return ST(e.matmul(ps[b4][:, 0:16], lhsT=onesf[:, :], rhs=yq[:, 0:16], start=(c == 0), stop=(c == 3)), "onesf")
                    P.pe(fn, reads=YK + ["ysq%d" % (c % 2), "onesf"], writes=[("ps", b3), ("ps", b4)])
                cgl.append((512, 16, b3, b4))
            for (o, nn, ba, bb) in cgl:
                P.dve(lambda e, o=o, nn=nn, ba=ba: e.tensor_scalar(out=mean[:, o:o + nn], in0=ps[ba][:, 0:nn], scalar1=1.0 / 512, scalar2=None, op0=ALU.mult),
                      reads=[("ps", ba)], writes=[("mean", o)])
                P.dve(lambda e, o=o, nn=nn: e.tensor_tensor(out=var[:, o:o + nn], in0=mean[:, o:o + nn], in1=mean[:, o:o + nn], op=ALU.mult),
                      reads=[("mean", o)], writes=[("var", o)])
                P.dve(lambda e, o=o, nn=nn, bb=bb: e.scalar_tensor_tensor(out=var[:, o:o + nn], in0=ps[bb][:, 0:nn], scalar=1.0 / 512, in1=var[:, o:o + nn],
                                                                          op0=ALU.mult, op1=ALU.subtract),
                      reads=[("ps", bb), ("var", o)], writes=[("var", o)])
                P.act(lambda e, o=o, nn=nn: e.activation(out=var[:, o:o + nn], in_=var[:, o:o + nn], func=AF.Sqrt, bias=epsc[:, 0:1]),
                      reads=[("var", o), "epsc"], writes=[("var", o)])
                P.dve(lambda e, o=o, nn=nn: e.reciprocal(out=var[:, o:o + nn], in_=var[:, o:o + nn]), reads=[("var", o)], writes=[("var", o)])
                for c in range(4):
                    yk = ("y", c, 0 if o == 0 else 1)
                    P.dve(lambda e, o=o, nn=nn, c=c: e.tensor_tensor(out=yb[:, c, o:o + nn], in0=yb[:, c, o:o + nn], in1=mean[:, o:o + nn], op=ALU.subtract),
                          reads=[yk, ("mean", o)], writes=[yk])
                    P.dve(lambda e, o=o, nn=nn, c=c: e.tensor_tensor(out=yb[:, c, o:o + nn], in0=yb[:, c, o:o + nn], in1=var[:, o:o + nn], op=ALU.mult),
                          reads=[yk, ("var", o)], writes=[yk])
                    P.act(lambda e, o=o, nn=nn, c=c: e.activation(out=sw[:, c, o:o + nn], in_=yb[:, c, o:o + nn], func=AF.Silu,
                                                                  scale=cvec[:, 1, c:c + 1], bias=cvec[:, 2, c:c + 1]),
                          reads=[yk, "cv1", "cv2"], writes=[("sw", c, o)])
            if not last:
                for c in range(4):
                    P.act(lambda e, c=c: e.activation(out=gluT[:, c, 0:30], in_=gluT[:, c, 30 + c0 + 482:30 + c0 + 512], func=AF.Copy),
                          reads=[("glu", c)], writes=[("gluc", c)])
            else:
                b = bank()

                def fn(e, b=b):
                    ins = None
                    for c in range(4):
                        ins = ST(e.transpose(ps[b][0:30, c * 128:(c + 1) * 128], glu32[:, c, 0:30], identf[:, :]), ("glu32m", c))
                    return ins
                P.pe(fn, reads=[("glu32m", c) for c in range(4)] + ["identf"], writes=[("ps", b)])
                P.act(lambda e, b=b: e.activation(out=otok[0:30, :], in_=ps[b][0:30, :], func=AF.Copy), reads=[("ps", b)], writes=["otok"])
                P.dma("sync", lambda e: e.dma_start(out=ncp, in_=otok[0:30, :]), reads=["otok"], writes=["o_ncp"])
                b = bank()

                def fn(e, b=b):
                    ins = None
                    for c in range(4):
                        ins = ST(e.transpose(ps[b][0:16, c * 128:(c + 1) * 128], glu32[:, c, 30:46], identf[:, :]), ("glu32s", c))
                    return ins
                P.pe(fn, reads=[("glu32s", c) for c in range(4)] + ["identf"], writes=[("ps", b)])
                P.act(lambda e, b=b: e.activation(out=otok[0:16, :], in_=ps[b][0:16, :], func=AF.Copy), reads=[("ps", b), "o_ncp"], writes=["otok"])
                P.dma("sync", lambda e: e.dma_start(out=ncs[:, 29, :], in_=otok[0:16, :]), reads=["otok"], writes=["o_ncs_new"])

        def post_stage(T):
            cgs = T["cgs_ms"]
            for i in range(8):
                k, s = w_use()
                wga = ring[:, s, 0:1024].rearrange("p (k n) -> p k n", k=8)
                wgc = ring[:, s, 1024:2048].rearrange("p (k n) -> p k n", k=8)
                wao = ring[:, s, 2048:2560].rearrange("p (j n) -> p j n", j=4)
                wco = ring[:, s, 2560:3072].rearrange("p (k n) -> p k n", k=4)
                for cg in cgs:
                    c0, n = cg
                    o = 0 if n == 512 else 512
                    so = c0 if n == 512 else 512
                    bA = bank(); bC = bank(); ba = bank(); bc = bank()

                    def fn(e, bA=bA, bC=bC, ba=ba, bc=bc, c0=c0, n=n, o=o, wga=wga, wgc=wgc, wao=wao, wco=wco, s=s):
                        ins = None
                        for kk in range(8):
                            ins = ST(e.matmul(ps[bA][:, 0:n], lhsT=wga[:, kk, :], rhs=UT[:, kk, c0:c0 + n], start=(kk == 0), stop=(kk == 7)), ("ring", s, 0))
                        for kk in range(8):
                            ins = ST(e.matmul(ps[bC][:, 0:n], lhsT=wgc[:, kk, :], rhs=UT[:, kk, c0:c0 + n], start=(kk == 0), stop=(kk == 7)), ("ring", s, 1))
                        for j in range(4):
                            ins = e.matmul(ps[ba][:, 0:n], lhsT=wao[:, j, :], rhs=aT[:, j, c0:c0 + n], start=(j == 0), stop=(j == 3))
                        for c in range(4):
                            ins = ST(e.matmul(ps[bc][:, 0:n], lhsT=wco[:, c, :], rhs=sw[:, c, o:o + n], start=(c == 0), stop=(c == 3)), ("ring", s, 4))
                        return ins
                    akeys = [("aT", s_["c0"]) for s_ in T["subs"] if s_["kind"] != "halo" and c0 <= s_["c0"] < c0 + n]
                    P.pe(fn, reads=RK(s) + ut_keys(T, cg) + akeys + [("sw", c, o) for c in range(4)],
                         writes=[("ps", bA), ("ps", bC), ("ps", ba), ("ps", bc)])
                    P.act(lambda e, bA=bA, n=n: e.activation(out=gA[:, 0:n], in_=ps[bA][:, 0:n], func=AF.Sigmoid), reads=[("ps", bA)], writes=["gA"])
                    P.act(lambda e, bC=bC, n=n: e.activation(out=gC[:, 0:n], in_=ps[bC][:, 0:n], func=AF.Sigmoid), reads=[("ps", bC)], writes=["gC"])
                    P.dve(lambda e, ba=ba, n=n: e.tensor_tensor(out=gA[:, 0:n], in0=gA[:, 0:n], in1=ps[ba][:, 0:n], op=ALU.mult),
                          reads=["gA", ("ps", ba)], writes=["gA"])
                    P.dve(lambda e, bc=bc, n=n: e.tensor_tensor(out=gC[:, 0:n], in0=gC[:, 0:n], in1=ps[bc][:, 0:n], op=ALU.mult),
                          reads=["gC", ("ps", bc)], writes=["gC"])
                    P.dve(lambda e, n=n, o=o, i=i: e.tensor_tensor(out=mT[:, i, o:o + n], in0=gA[:, 0:n], in1=gC[:, 0:n], op=ALU.add),
                          reads=["gA", "gC"], writes=[("mT", i, o)])
                w_release(k)
            ws = [w_use() for _ in range(3)]
            spans = ((0, 384), (384, 768), (768, 1024))
            for sub in T["subs"]:
                if sub["kind"] == "halo":
                    continue
                nt = sub["n"]
                o = 512 if sub["kind"] == "samp" else sub["c0"] - T["off"]
                og = 512 if sub["kind"] == "samp" else 0
                for (k, s), (a, bnd) in zip(ws, spans):
                    wdt = bnd - a
                    b = bank()
                    wv = ring[:, s, 0:8 * wdt].rearrange("p (k n) -> p k n", k=8)

                    def fn(e, b=b, nt=nt, o=o, wv=wv, wdt=wdt, og=og):
                        ins = None
                        for kk in range(8):
                            ins = ST(e.matmul(ps[b][0:nt, 0:wdt], lhsT=mT[:, kk, o:o + nt], rhs=wv[:, kk, :], start=(kk == 0), stop=(kk == 7)), ("mT", kk, og))
                        return ins
                    P.pe(fn, reads=RK(s) + [("mT", i, og) for i in range(8)], writes=[("ps", b)])
                    P.dve(lambda e, b=b, nt=nt, sub=sub, a=a, bnd=bnd, wdt=wdt: e.tensor_tensor(
                        out=H[0:nt, sub["h"], a:bnd], in0=ps[b][0:nt, 0:wdt], in1=H[0:nt, sub["h"], a:bnd], op=ALU.add),
                        reads=[("ps", b), ("H", sub["h"])], writes=[("H", sub["h"])])
            for (k, s) in ws:
                w_release(k)

        for T in tiles[:MAXT]:
            t = T["t"]
            lastT = (t == MAXT - 1)
            def on(k):
                return (not lastT) or STAGES >= k
            for sub in T["subs"]:
                if sub["kind"] == "halo":
                    src = xh
                elif sub["kind"] == "samp":
                    src = xs
                else:
                    src = xm[sub["blk"] * 128:(sub["blk"] + 1) * 128, :]
                P.dma("sync", lambda e, sub=sub, src=src: e.dma_start(out=H[0:sub["n"], sub["h"], :], in_=src), writes=[("H", sub["h"])])
            def dbg_dump(dst):
                for sub in T["subs"]:
                    if sub["kind"] == "main":
                        P.dma("sync", lambda e, sub=sub: e.dma_start(out=dst[sub["blk"] * 128:(sub["blk"] + 1) * 128, :], in_=H[:, sub["h"], :]),
                              reads=[("H", sub["h"])], writes=[("dbg", id(dst), sub["blk"])])
            if DEBUG_OUT:
                dbg_dump(dbg3)
            norm_stage(T, 0, T["subs"])
            if t == 0:
                build_bias()
            if on(1):
                ffn_stage(T, T["cgs_all"], T["subs"])
            if DEBUG_OUT:
                dbg_dump(dbg1)
            if not on(2):
                continue
            norm_stage(T, 1, T["subs"])
            def dump(name, tens, shape, dt, rkeys):
                if not DEBUG_OUT:
                    return
                if name not in DEBUG_NAMES:
                    return
                d = nc.dram_tensor("dbg_%s_%d" % (name, t), list(shape), F32, kind="ExternalOutput").ap()
                P.dma("pool", lambda e: e.dma_start(out=d, in_=tens), reads=rkeys, writes=[("dbgx", name, t)])
            qkv_stage(T)
            if not on(3):
                continue
            dump("qT", qT[:, :, :], [128, 4, 640], BF16, [("qT", s_["c0"]) for s_ in T["subs"]])
            dump("kT", kT[:, :], [128, 768], BF16, [("kT", i) for i in range(6)])
            dump("Vb", Vb[:, :, :], [128, 6, 128], BF16, [("V", i) for i in range(6)])
            glu_stage(T)
            if not on(4):
                continue
            dump("glu", gluT[:, :, :], [128, 4, 670], BF16, [("glu", c) for c in range(4)] + [("gluc", c) for c in range(4)])
            attn_stage(T)
            if not on(5):
                continue
            if t == 3:
                sample_attn()
            dump("aT", aT[:, :, :], [128, 4, 640], BF16, [("aT", s_["c0"]) for s_ in T["subs"]])
            if not on(6):
                continue
            conv_stage(T)
            if not on(7):
                continue
            dump("sw", sw[:, :, :], [128, 4, 528], BF16, [("sw", c, o) for c in range(4) for o in (0, 512)])
            post_stage(T)
            if not on(8):
                continue
            dump("mT", mT[:, :, :], [128, 8, 528], BF16, [("mT", i, o) for i in range(8) for o in (0, 512)])
            if DEBUG_OUT:
                dbg_dump(dbg2)
            subs2 = [s for s in T["subs"] if s["kind"] != "halo"]
            norm_stage(T, 2, subs2)
            ffn_stage(T, T["cgs_ms"], subs2)
            for sub in subs2:
                if sub["kind"] == "samp":
                    dst = y_s
                else:
                    dst = y_m[sub["blk"] * 128:(sub["blk"] + 1) * 128, :]
                P.dma("sync", lambda e, sub=sub, dst=dst: e.dma_start(out=dst, in_=H[0:sub["n"], sub["h"], :]),
                      reads=[("H", sub["h"])], writes=[("o_y", sub["h"])])
        assert MAXT < 4 or STAGES < 99 or cur[0] == n_pieces, (cur[0], n_pieces)
        _NC_CACHE['P'] = P
        P.emit(es)
    return nc


_NC_CACHE = {}


def _t5_bucket_np(d):
    import math
    max_exact = 16
    d = np.maximum(d, 0)
    ratio = np.log(np.maximum(d, 1).astype(np.float32) / max_exact) / np.float32(math.log(128 / max_exact))
    large = np.minimum(max_exact + (ratio * (32 - max_exact)).astype(np.int32), 31)
    return np.where(d < max_exact, d, large)


def _consts():
    identb = np.eye(128, dtype=np.float32).astype(ml_dtypes.bfloat16)
    identf = np.eye(128, dtype=np.float32)
    er = np.zeros((33, 383), np.float32)
    for m in range(383):
        d = m - 127
        if 0 <= d < 128:
            er[int(_t5_bucket_np(np.array([d]))[0]), m] = 1.0
        else:
            er[32, m] = 1.0
    sel = np.zeros((120, 4), np.float32)
    for p in range(120):
        sel[p, p // 30] = 1.0
    return identb, identf, er, sel


def make_in_maps(inputs):
    f = lambda k: np.ascontiguousarray(np.asarray(inputs[k], dtype=np.float32))
    x_prompt = f("x_prompt"); x_sample = f("x_sample"); cache_k = f("cache_k"); cache_v = f("cache_v")
    state_conv = f("state_conv"); meta = f("meta_tokens")
    identb, identf, er, sel = _consts()
    wnames = ["ffn1_norm", "ffn1_w1", "ffn1_w3", "ffn1_w2", "mix_norm", "w_in", "q_norm", "k_norm", "rel_bias", "sinks",
              "w_attn_out", "w_dw", "b_dw", "conv_ln_g", "conv_ln_b", "w_conv_out", "w_out", "ffn2_norm", "ffn2_w1",
              "ffn2_w3", "ffn2_w2"]
    wts = {k: f(k) for k in wnames}
    in_maps = []
    for c in range(8):
        b, cc = divmod(c, 4)
        m = dict(wts)
        m["xm"] = np.ascontiguousarray(x_prompt[b, cc * 2048:(cc + 1) * 2048])
        if cc == 0:
            xh = np.zeros((128, D), np.float32)
            xh[112:] = meta
            hm = np.zeros((128, 1), np.float32)
            hm[:112] = NEG
        else:
            xh = np.ascontiguousarray(x_prompt[b, cc * 2048 - 128:cc * 2048])
            hm = np.zeros((128, 1), np.float32)
        m["xh"] = xh
        m["hmask"] = hm
        m["xs"] = np.ascontiguousarray(x_sample[c * 16:(c + 1) * 16, 0, :])
        m["ck"] = np.ascontiguousarray(cache_k[c * 16:(c + 1) * 16].reshape(16, 128, 128))
        m["cv"] = np.ascontiguousarray(cache_v[c * 16:(c + 1) * 16].reshape(16, 128, 128))
        m["stc"] = np.ascontiguousarray(state_conv[c * 16:(c + 1) * 16])
        m["identb"] = identb; m["identf"] = identf; m["er"] = er; m["sel"] = sel
        in_maps.append(m)
    return in_maps


def kernel(**inputs):
    if "nc" not in _NC_CACHE:
        _NC_CACHE["nc"] = build_nc()
    nc = _NC_CACHE["nc"]
    in_maps = make_in_maps(inputs)
    res = run_bass_kernel_spmd(nc, in_maps, core_ids=list(range(8)))
    R = res.results
    y_prompt = np.stack([np.concatenate([R[b * 4 + cc]["y_m"] for cc in range(4)], axis=0) for b in range(2)], axis=0)
    y_sample = np.concatenate([R[c]["y_s"] for c in range(8)], axis=0).reshape(128, 1, D)
    nkp = np.stack([R[3]["nkp"], R[7]["nkp"]], axis=0).reshape(2, 128, 2, 64)
    nvp = np.stack([R[3]["nvp"], R[7]["nvp"]], axis=0).reshape(2, 128, 2, 64)
    ncp = np.stack([R[3]["ncp"], R[7]["ncp"]], axis=0)
    nks = np.concatenate([R[c]["nks"] for c in range(8)], axis=0).reshape(128, 128, 2, 64)
    nvs = np.concatenate([R[c]["nvs"] for c in range(8)], axis=0).reshape(128, 128, 2, 64)
    ncs = np.concatenate([R[c]["ncs"] for c in range(8)], axis=0)
    return tuple(np.asarray(a, dtype=np.float32) for a in (y_prompt, y_sample, nkp, nvp, ncp, nks, nvs, ncs))
```
